# Optimizing a Trainium2 kernel written in Bass

```python
import math
import jax
import jax.numpy as jnp
from jax import lax
import numpy as np

D_MODEL = 1024
BATCH = 2
SEQ = 8192
DEPTH = 4
DEC_BATCH = 128
DEC_SEQ = 8
PAST_LEN = 8192
PAGE_SIZE = 128

HEAD_DIM = 64
ATTN_HEADS = D_MODEL // 128
KV_HEADS = max(1, ATTN_HEADS // 4)
Q_PER_KV = ATTN_HEADS // KV_HEADS
ATTN_DIM = ATTN_HEADS * HEAD_DIM
KV_DIM = KV_HEADS * HEAD_DIM
WINDOW = 128
ATTN_BLOCK = WINDOW
MEM_HEADS = 4
MEM_DIM = MEM_HEADS * HEAD_DIM
N_MEM = 256
CONV_DIM = D_MODEL - ATTN_DIM - MEM_DIM
CONV_W = 3
CONV_BUF = CONV_W - 1
MIX_DIM = CONV_DIM + ATTN_DIM + MEM_DIM
IN_DIM = 4 * CONV_DIM + 2 * ATTN_DIM + 2 * KV_DIM + 2 * MEM_DIM
RMS_EPS = 1e-6

kernel_name = "hymba_conv_swa_sink_memxattn_step"


def _rmsnorm(x, g):
    xf = x.astype(jnp.float32)
    r = lax.rsqrt(jnp.mean(xf * xf, axis=-1, keepdims=True) + RMS_EPS)
    return (xf * r).astype(x.dtype) * g


def _split_cols(z):
    sizes = (CONV_DIM, CONV_DIM, CONV_DIM, CONV_DIM,
             ATTN_DIM, KV_DIM, KV_DIM, ATTN_DIM, MEM_DIM, MEM_DIM)
    idx, acc = [], 0
    for s in sizes[:-1]:
        acc += s
        idx.append(acc)
    return jnp.split(z, idx, axis=-1)


def _sink_attend(q, k, v, sink, mask):
    s = jnp.einsum('...qkgd,...skd->...kgqs', q, k).astype(jnp.float32) * (HEAD_DIM ** -0.5)
    s = jnp.where(mask, s, -jnp.inf)
    sk = sink.astype(jnp.float32)[:, :, None]
    m = jnp.maximum(jnp.max(s, axis=-1), sk)
    p = jnp.exp(s - m[..., None])
    p = p / (jnp.sum(p, axis=-1) + jnp.exp(sk - m))[..., None]
    return jnp.einsum('...kgqs,...skd->...qkgd', p.astype(v.dtype), v)


def _window_prompt(q, k, v, sink):
    b, s = q.shape[0], q.shape[1]
    nb = s // ATTN_BLOCK
    qb = q.reshape(b, nb, ATTN_BLOCK, KV_HEADS, Q_PER_KV, HEAD_DIM)
    kb = k.reshape(b, nb, ATTN_BLOCK, KV_HEADS, HEAD_DIM)
    vb = v.reshape(b, nb, ATTN_BLOCK, KV_HEADS, HEAD_DIM)
    kk = jnp.concatenate([jnp.concatenate([jnp.zeros_like(kb[:, :1]), kb[:, :-1]], axis=1), kb], axis=2)
    vv = jnp.concatenate([jnp.concatenate([jnp.zeros_like(vb[:, :1]), vb[:, :-1]], axis=1), vb], axis=2)
    a = jnp.arange(ATTN_BLOCK)[:, None]
    j = jnp.arange(2 * ATTN_BLOCK)[None, :]
    diff = a + ATTN_BLOCK - j
    blk = jnp.arange(nb)[:, None, None]
    valid_key = (blk * ATTN_BLOCK + j[None] - ATTN_BLOCK) >= 0
    mask = (diff >= 0)[None] & (diff < WINDOW)[None] & valid_key
    o = _sink_attend(qb, kk, vv, sink, mask[None, :, None, None])
    return o.reshape(b, s, ATTN_DIM)


def _window_sample(q, k, v, buf_k, buf_v, sink):
    n, t = q.shape[0], q.shape[1]
    kk = jnp.concatenate([buf_k, k], axis=1)
    vv = jnp.concatenate([buf_v, v], axis=1)
    i = jnp.arange(t)[:, None]
    j = jnp.arange(WINDOW + t)[None, :]
    diff = i + WINDOW - j
    mask = (diff >= 0) & (diff < WINDOW)
    o = _sink_attend(q, kk, vv, sink, mask)
    return o.reshape(n, t, ATTN_DIM), kk[:, -WINDOW:], vv[:, -WINDOW:]


def _mem_attend(q, mk, mv):
    s = jnp.einsum('nthd,nmhd->nhtm', q, mk).astype(jnp.float32) * (HEAD_DIM ** -0.5)
    p = jax.nn.softmax(s, axis=-1)
    o = jnp.einsum('nhtm,nmhd->nthd', p.astype(mv.dtype), mv)
    return o.reshape(q.shape[0], q.shape[1], MEM_DIM)


def _layer(x, conv_buf, buf_k, buf_v, mem_k, mem_v, g_pre, g_post, w_in, conv_w, sink, w_out, prompt):
    n, t = x.shape[0], x.shape[1]
    h = _rmsnorm(x, g_pre)
    z = h @ w_in
    cb, cc, ch, cg, q, k, v, ag, mq, mg = _split_cols(z)
    u = cc * ch
    up = jnp.concatenate([conv_buf, u], axis=1)
    conv = conv_w[0] * up[:, 0:t] + conv_w[1] * up[:, 1:t + 1] + conv_w[2] * up[:, 2:t + 2]
    out_a = jax.nn.silu(cg) * cb * conv
    new_conv = up[:, -CONV_BUF:]
    q = q.reshape(n, t, KV_HEADS, Q_PER_KV, HEAD_DIM)
    k = k.reshape(n, t, KV_HEADS, HEAD_DIM)
    v = v.reshape(n, t, KV_HEADS, HEAD_DIM)
    sink_g = sink.reshape(KV_HEADS, Q_PER_KV)
    if prompt:
        o_b = _window_prompt(q, k, v, sink_g)
        new_k, new_v = k[:, -WINDOW:], v[:, -WINDOW:]
    else:
        o_b, new_k, new_v = _window_sample(q, k, v, buf_k, buf_v, sink_g)
    out_b = jax.nn.silu(ag) * o_b
    o_c = _mem_attend(mq.reshape(n, t, MEM_HEADS, HEAD_DIM), mem_k, mem_v)
    out_c = jax.nn.silu(mg) * o_c
    y = jnp.concatenate([out_a, out_b, out_c], axis=-1) @ w_out
    return x + _rmsnorm(y, g_post), new_conv, new_k, new_v


def setup_inputs(seed: int = 0) -> dict:
    key = jax.random.key(seed)
    ks = jax.random.split(key, 20)
    f32 = jnp.float32
    nrm = lambda k, shp, sc: jax.random.normal(k, shp, f32) * sc
    return {
        "x_prompt": nrm(ks[0], (BATCH, SEQ, D_MODEL), 1.0),
        "x_sample": nrm(ks[1], (DEC_BATCH, DEC_SEQ, D_MODEL), 1.0),
        "mem_prompt": nrm(ks[2], (BATCH, N_MEM, D_MODEL), 1.0),
        "cache_win_k": nrm(ks[3], (DEPTH, DEC_BATCH, WINDOW, KV_HEADS, HEAD_DIM), 1.0),
        "cache_win_v": nrm(ks[4], (DEPTH, DEC_BATCH, WINDOW, KV_HEADS, HEAD_DIM), 1.0),
        "state_conv": nrm(ks[5], (DEPTH, DEC_BATCH, CONV_BUF, CONV_DIM), 1.0),
        "cache_mem_k": nrm(ks[6], (DEPTH, DEC_BATCH, N_MEM, MEM_HEADS, HEAD_DIM), 1.0),
        "cache_mem_v": nrm(ks[7], (DEPTH, DEC_BATCH, N_MEM, MEM_HEADS, HEAD_DIM), 1.0),
        "norm_pre": 1.0 + nrm(ks[8], (DEPTH, D_MODEL), 0.05),
        "norm_post": 1.0 + nrm(ks[9], (DEPTH, D_MODEL), 0.05),
        "norm_mem": 1.0 + nrm(ks[10], (DEPTH, D_MODEL), 0.05),
        "w_in": nrm(ks[11], (DEPTH, D_MODEL, IN_DIM), D_MODEL ** -0.5),
        "conv_w": nrm(ks[12], (DEPTH, CONV_W, CONV_DIM), CONV_W ** -0.5),
        "attn_sinks": nrm(ks[13], (DEPTH, ATTN_HEADS), 0.5),
        "w_mem_kv": nrm(ks[14], (DEPTH, D_MODEL, 2 * MEM_DIM), D_MODEL ** -0.5),
        "w_out": nrm(ks[15], (DEPTH, MIX_DIM, D_MODEL), MIX_DIM ** -0.5),
    }


def reference(x_prompt, x_sample, mem_prompt, cache_win_k, cache_win_v, state_conv,
              cache_mem_k, cache_mem_v, norm_pre, norm_post, norm_mem, w_in, conv_w,
              attn_sinks, w_mem_kv, w_out):
    xp, xs = x_prompt, x_sample
    bp, m = mem_prompt.shape[0], mem_prompt.shape[1]
    wkp, wvp, cvp, mkp, mvp, wks, wvs, cvs = [], [], [], [], [], [], [], []
    for l in range(DEPTH):
        mkv = _rmsnorm(mem_prompt, norm_mem[l]) @ w_mem_kv[l]
        mk = mkv[..., :MEM_DIM].reshape(bp, m, MEM_HEADS, HEAD_DIM)
        mv = mkv[..., MEM_DIM:].reshape(bp, m, MEM_HEADS, HEAD_DIM)
        zero_buf = jnp.zeros((xp.shape[0], CONV_BUF, CONV_DIM), xp.dtype)
        xp, cb_p, k_p, v_p = _layer(xp, zero_buf, None, None, mk, mv, norm_pre[l], norm_post[l],
                                    w_in[l], conv_w[l], attn_sinks[l], w_out[l], True)
        xs, cb_s, k_s, v_s = _layer(xs, state_conv[l], cache_win_k[l], cache_win_v[l],
                                    cache_mem_k[l], cache_mem_v[l], norm_pre[l], norm_post[l],
                                    w_in[l], conv_w[l], attn_sinks[l], w_out[l], False)
        wkp.append(k_p); wvp.append(v_p); cvp.append(cb_p); mkp.append(mk); mvp.append(mv)
        wks.append(k_s); wvs.append(v_s); cvs.append(cb_s)
    return (xp, xs, jnp.stack(wkp), jnp.stack(wvp), jnp.stack(cvp), jnp.stack(mkp), jnp.stack(mvp),
            jnp.stack(wks), jnp.stack(wvs), jnp.stack(cvs))
```

```python
import numpy as np
import concourse.bass as bass
import concourse.mybir as mybir
from concourse.bass_utils import run_bass_kernel_spmd

F32 = mybir.dt.float32
BF16 = mybir.dt.bfloat16
AF = mybir.ActivationFunctionType
ALU = mybir.AluOpType

NCORES = 8
DEPTH = 4
DM = 1024
IN_DIM = 2816
T = 256
NHALO = 2
NOWN = 8
NPC = NHALO + NOWN
XS0 = NPC * T
TS = 128
XCOLS = XS0 + TS
NSEQ = 16


class _Rec:
    def __getattr__(self, name):
        def f(*args, **kw):
            self.call = (name, args, kw)
            return self
        return f


class Sched:
    ENGS = ("pe", "act", "dve", "pool", "sp")

    def __init__(self, nc):
        self.nc = nc
        self.eng_obj = {"pe": nc.tensor, "act": nc.scalar, "dve": nc.vector,
                        "pool": nc.gpsimd, "sp": nc.sync}
        self.instrs = []
        self.last_write = {}
        self.reads_since = {}
        self.dma_sem_count = {}
        self.dma_group = {}
        self.final_dma = []

    def _add(self, eng, emit, reads, writes, dma_key=None):
        i = len(self.instrs)
        rp = _Rec()
        emit(rp)
        emit = rp.call
        deps = set()
        for r in reads:
            deps.update(self.last_write.get(r, ()))
        par_dma = {}
        for r in writes:
            lw = self.last_write.get(r, [])
            rs = self.reads_since.get(r, [])
            if dma_key is not None and lw and not rs and all(self.instrs[w]["dma_key"] is not None for w in lw):
                par_dma[r] = True
                continue
            deps.update(lw)
            deps.update(rs)
        rec = dict(eng=eng, emit=emit, deps=deps, dma_key=dma_key, dma_val=None,
                   need_inc=False, tick=None)
        if dma_key is not None:
            c = self.dma_sem_count.get(dma_key, 0) + 1
            self.dma_sem_count[dma_key] = c
            rec["dma_val"] = 16 * c
            self.dma_group.setdefault(dma_key, []).append(i)
        self.instrs.append(rec)
        for r in reads:
            self.reads_since.setdefault(r, []).append(i)
        for r in writes:
            if r in par_dma:
                self.last_write[r].append(i)
            else:
                self.last_write[r] = [i]
                self.reads_since[r] = []
        return i

    def op(self, eng, emit, reads=(), writes=()):
        writes = tuple(writes) + tuple(r for r in reads if isinstance(r, tuple) and r[0] == "PS" and r not in writes)
        return self._add(eng, emit, tuple(reads), writes)

    def dma(self, queue, emit, reads=(), writes=(), key=None, final=False):
        i = self._add(queue, emit, tuple(reads), tuple(writes), dma_key=key)
        if final:
            self.final_dma.append(i)
        return i

    def seal(self, key):
        tot = 16 * self.dma_sem_count.get(key, 0)
        for i in self.dma_group.get(key, []):
            self.instrs[i]["dma_val"] = tot
        self.dma_group[key] = []

    def finalize(self):
        nc = self.nc
        ins = self.instrs
        for rec in ins:
            for d in rec["deps"]:
                p = ins[d]
                if p["dma_key"] is None and p["eng"] != rec["eng"]:
                    p["need_inc"] = True
        cnt = {e: 0 for e in self.ENGS}
        for rec in ins:
            if rec["dma_key"] is None and rec["need_inc"]:
                cnt[rec["eng"]] += 1
                rec["tick"] = cnt[rec["eng"]]
        eng_sem = {e: nc.alloc_semaphore(name=f"s_{e}") for e in self.ENGS if cnt[e] > 0}
        dma_sem = {k: nc.alloc_semaphore(name=f"d_{i}") for i, k in enumerate(self.dma_sem_count)}
        per_eng = {e: [] for e in self.ENGS}
        for idx, rec in enumerate(ins):
            per_eng[rec["eng"]].append(idx)

        def emit_engine(e):
            eo = self.eng_obj[e]
            waited = {}
            for idx in per_eng[e]:
                rec = ins[idx]
                need = {}
                for d in rec["deps"]:
                    p = ins[d]
                    if p["dma_key"] is not None:
                        k = ("d", p["dma_key"])
                        need[k] = max(need.get(k, 0), p["dma_val"])
                    elif p["eng"] != e:
                        k = ("c", p["eng"])
                        need[k] = max(need.get(k, 0), p["tick"])
                for k, v in need.items():
                    if waited.get(k, 0) >= v:
                        continue
                    waited[k] = v
                    sem = dma_sem[k[1]] if k[0] == "d" else eng_sem[k[1]]
                    eo.wait_ge(sem, v)
                fname, fargs, kw = rec["emit"]
                r = getattr(eo, fname)(*fargs, **kw)
                if rec["dma_key"] is not None:
                    r.then_inc(dma_sem[rec["dma_key"]], 16)
                elif rec["need_inc"]:
                    r.then_inc(eng_sem[e], 1)
            if e == "sp":
                for k, c in self.dma_sem_count.items():
                    eo.wait_ge(dma_sem[k], 16 * c)

        with nc.Block() as block:
            @block.tensor
            def _(x):
                emit_engine("pe")

            @block.scalar
            def _(x):
                emit_engine("act")

            @block.vector
            def _(x):
                emit_engine("dve")

            @block.gpsimd
            def _(x):
                emit_engine("pool")

            @block.sync
            def _(x):
                emit_engine("sp")
        return {e: len(per_eng[e]) for e in self.ENGS}


def build_nc(depth=DEPTH):
    nc = bass.Bass("TRN2", target_bir_lowering=False)

    def din(name, shape):
        return nc.dram_tensor(name, list(shape), F32, kind="ExternalInput").ap()

    def dout(name, shape):
        return nc.dram_tensor(name, list(shape), F32, kind="ExternalOutput").ap()

    xp_T = din("xp_T", [DM, NPC * T])
    xs_T = din("xs_T", [DM, TS])
    mem_T = din("mem_T", [DM, 256])
    ckw = din("ckw", [DEPTH, NSEQ, 128, 128])
    cvw = din("cvw", [DEPTH, NSEQ, 128, 128])
    cmk = din("cmk", [DEPTH, NSEQ, 256, 256])
    cmv = din("cmv", [DEPTH, NSEQ, 256, 256])
    sconv = din("sconv", [DEPTH, 256, 32])
    gains = din("gains", [128, 3 * DEPTH * 8])
    convw = din("convw", [128, DEPTH * 6])
    sinks = din("sinks", [128, DEPTH * 8])
    w_in = din("w_in", [DEPTH, DM, IN_DIM])
    w_out = din("w_out", [DEPTH, DM, DM])
    w_mem = din("w_mem", [DEPTH, DM, 512])
    masks = din("masks", [128, 5 * 128])
    ident = din("ident", [128, 128])

    y_T = dout("y_T", [DM, NOWN * T])
    ys_T = dout("ys_T", [DM, TS])
    wk_p = dout("wk_p", [DEPTH, 128, 128])
    wv_p = dout("wv_p", [DEPTH, 128, 128])
    conv_p = dout("conv_p", [DEPTH, 256, 2])
    mk_p = dout("mk_p", [DEPTH, 256, 256])
    mv_p = dout("mv_p", [DEPTH, 256, 256])
    wk_s = dout("wk_s", [DEPTH, NSEQ, 128, 128])
    wv_s = dout("wv_s", [DEPTH, NSEQ, 128, 128])
    conv_s = dout("conv_s", [DEPTH, 256, 32])

    win_bf = nc.dram_tensor("win_bf", [DEPTH - 1, DM, IN_DIM], BF16, kind="Internal").ap()
    wout_bf = nc.dram_tensor("wout_bf", [DEPTH - 1, DM, DM], BF16, kind="Internal").ap()

    S = Sched(nc)

    def sb(name, shape, dt):
        return nc.alloc_sbuf_tensor(name, list(shape), dt)

    X = sb("X", [128, 8, XCOLS], F32)
    WI = sb("WI", [128, 8, IN_DIM], BF16)
    WO = sb("WO", [128, 8, DM], BF16)
    SQ = sb("SQ", [128, 8, T], BF16)
    H = sb("H", [128, 8, T], BF16)
    Y = sb("Y", [128, 8, T], F32)
    YSQ = SQ
    RS = sb("RS", [128, T], F32)
    RS2 = sb("RS2", [128, T], F32)
    U = sb("U", [128, 2, T + 2], F32)
    US = sb("US", [128, 2, 160], F32)
    CB = sb("CB", [128, 2, T], BF16)
    SG = sb("SG", [128, 2, T], BF16)
    ACC = sb("ACC", [128, 2, T], F32)
    QT = sb("QT", [128, 4, T], BF16)
    KT = sb("KT", [128, 4, 128], BF16)
    VT = sb("VT", [128, 4, 2, 128], BF16)
    SAG = sb("SAG", [128, 4, T], BF16)
    SMG = sb("SMG", [128, 2, T], BF16)
    MQ = sb("MQ", [128, 2, T], BF16)
    MIX = sb("MIX", [128, 8, T], BF16)
    NPT = 4
    PT = [sb(f"PT{i}", [128, 512], BF16) for i in range(NPT)]
    RC = sb("RC", [128, 512], F32)
    TMP = sb("TMP", [128, 512], BF16)
    MKT = sb("MKT", [128, 2, 256], BF16)
    MV = sb("MV", [128, 2, 4, 128], BF16)
    ONES = sb("ONES", [128, 128], BF16)
    IDB = sb("IDB", [128, 128], BF16)
    MASKS = sb("MASKS", [128, 5, 128], BF16)
    GAINS = sb("GAINS", [128, 3, DEPTH, 8], F32)
    CW = sb("CW", [128, DEPTH, 2, 3], F32)
    ESINK = sb("ESINK", [128, DEPTH * 8], F32)
    EPS = sb("EPS", [128, 1], F32)
    KCS = [sb(f"KCS{i}", [128, 128], F32) for i in range(2)]
    IDF = sb("IDF", [128, 128], F32)
    KCT = [sb(f"KCT{i}", [128, 128], BF16) for i in range(2)]
    VC = [sb(f"VC{i}", [128, 128], BF16) for i in range(2)]
    MKS = [sb(f"MKS{i}", [128, 2, 256], F32) for i in range(2)]
    MKTS = [sb(f"MKTS{i}", [128, 2, 2, 128], BF16) for i in range(2)]
    MVS = [sb(f"MVS{i}", [128, 2, 256], BF16) for i in range(2)]
    PSS = [sb(f"PSS{i}", [128, 128], BF16) for i in range(2)]

    banks = [nc.alloc_psum_tensor(f"ps{i}", [128, 512], F32) for i in range(8)]
    ring = {"list": list(range(8)), "pos": 0}

    def newbank():
        lst = ring["list"]
        i = lst[ring["pos"] % len(lst)]
        ring["pos"] += 1
        return banks[i], ("PS", i)

    ptpos = [0]
    ptring = {"n": 8}
    PTX = [(PT[i][:, :], ("PT", i)) for i in range(NPT)]
    PTX += [(MKS[i][:].rearrange("p a b -> p (a b)").bitcast(BF16)[:, 0:512], ("MKS", i)) for i in range(2)]
    PTX += [(MKTS[i][:].rearrange("p a b c -> p (a b c)"), ("MKTS", i)) for i in range(2)]

    def newpt():
        i = ptpos[0] % ptring["n"]
        ptpos[0] += 1
        return PTX[i]

    def v3(ap2d, a):
        return ap2d.rearrange("p (a b) -> p a b", a=a)

    S.op("dve", lambda e: e.memset(ONES[:], 1.0), writes=["ONES"])
    S.op("dve", lambda e: e.memset(EPS[:], 1e-6), writes=["EPS"])
    S.op("dve", lambda e: e.memset(KT[:], 0.0), writes=[("KT", i) for i in range(4)])
    S.op("dve", lambda e: e.memset(VT[:], 1.0), writes=[("VT", i) for i in range(4)])
    S.op("dve", lambda e: e.memset(MV[:], 1.0), writes=["MV"])
    S.op("dve", lambda e: e.memset(U[:], 0.0), writes=["U"])
    S.op("dve", lambda e: e.memset(US[:], 0.0), writes=["US"])
    S.dma("sp", lambda e: e.dma_start(out=GAINS[:], in_=gains.rearrange("p (a l k) -> p a l k", a=3, l=DEPTH)),
          writes=["GAINS"], key=("init", 1))
    S.dma("sp", lambda e: e.dma_start(out=CW[:], in_=convw.rearrange("p (l i j) -> p l i j", l=DEPTH, i=2)),
          writes=["CW"], key=("init", 2))
    S.dma("sp", lambda e: e.dma_start(out=ESINK[:], in_=sinks), writes=["ESINK"], key=("init", 3))
    S.dma("pool", lambda e: e.dma_start(out=MASKS[:], in_=masks.rearrange("p (a b) -> p a b", a=5)),
          writes=["MASKS"], key=("init", 4))
    S.dma("pool", lambda e: e.dma_start(out=IDB[:], in_=ident), writes=["IDB"], key=("init", 5))
    S.dma("sp", lambda e: e.dma_start(out=IDF[:], in_=ident), writes=["IDF"], key=("init", 6))
    S.op("act", lambda e: e.activation(out=ESINK[:], in_=ESINK[:], func=AF.Exp), reads=["ESINK"], writes=["ESINK"])

    xp_v = xp_T.rearrange("(k p) t -> p k t", p=128)
    xs_v = xs_T.rearrange("(k p) t -> p k t", p=128)

    def load_x(c):
        if c < NPC:
            S.dma("sp", lambda e: e.dma_start(out=X[:, :, c * T:(c + 1) * T], in_=xp_v[:, :, c * T:(c + 1) * T]),
                  writes=[("X", c * T)], key=("x", c))
        else:
            S.dma("sp", lambda e: e.dma_start(out=X[:, :, XS0:XS0 + TS], in_=xs_v), writes=[("X", XS0)], key=("x", NPC))

    def load_rest_inputs():
        for c in range(1, NPC + 1):
            load_x(c)
        for l in range(depth):
            S.dma("sp", lambda e: e.dma_start(out=wk_s[l, :, 0:120, :], in_=ckw[l, :, 8:128, :]), key="cpyk", final=True)
            S.dma("sp", lambda e: e.dma_start(out=wv_s[l, :, 0:120, :], in_=cvw[l, :, 8:128, :]), key="cpyv", final=True)

    load_x(0)

    WI_PIECES = ((1024, 1792), (2304, 2816), (0, 1024), (1792, 2304))

    def wi_piece(col):
        for pi_, (c0, c1) in enumerate(WI_PIECES):
            if c0 <= col < c1:
                return pi_
        raise ValueError(col)

    def load_wi_piece(l, pi_, k):
        c0, c1 = WI_PIECES[pi_]
        if l == 0:
            S.dma("pool", lambda e: e.dma_start(out=WI[:, k, c0:c1], in_=w_in[l, k * 128:(k + 1) * 128, c0:c1]),
                  writes=[("WI", pi_)], key=("WI", pi_, k))
        else:
            S.dma("sp", lambda e: e.dma_start(out=WI[:, k, c0:c1], in_=win_bf[l - 1, k * 128:(k + 1) * 128, c0:c1]),
                  reads=[("WSCR", l)], writes=[("WI", pi_)], key=("WI", pi_, k))

    def precast(l, after):
        for k in range(8):
            for c0, c1 in ((0, 1024), (1024, 2048), (2048, IN_DIM)):
                S.dma("pool", lambda e: e.dma_start(out=win_bf[l - 1, k * 128:(k + 1) * 128, c0:c1],
                                                    in_=w_in[l, k * 128:(k + 1) * 128, c0:c1]),
                      reads=[after], writes=[("WSCR", l)], key=("pc", l))
            S.dma("pool", lambda e: e.dma_start(out=wout_bf[l - 1, k * 128:(k + 1) * 128, :], in_=w_out[l, k * 128:(k + 1) * 128, :]),
                  reads=[after], writes=[("WSCR", l)], key=("pc", l))
        S.seal(("pc", l))

    def load_wi(l, k):
        for pi_ in range(4):
            load_wi_piece(l, pi_, k)

    def load_wo(l):
        for k in range(8):
            if l == 0:
                S.dma("pool", lambda e, k=k: e.dma_start(out=WO[:, k, :], in_=w_out[l, k * 128:(k + 1) * 128, :]),
                      writes=["WO"], key=("WOk", k))
            else:
                S.dma("sp", lambda e, k=k: e.dma_start(out=WO[:, k, :], in_=wout_bf[l - 1, k * 128:(k + 1) * 128, :]),
                      reads=[("WSCR", l)], writes=["WO"], key=("WOk", k))

    def rms_stage(srcs_sq, sqbuf, sqres, rsbuf, rsres, Tn):
        pb, pr = newbank()
        for k in range(8):
            S.op("pe", lambda e, k=k: e.matmul(pb[:, 0:Tn], lhsT=ONES[:, :], rhs=sqbuf[:, k, 0:Tn],
                                                start=(k == 0), stop=(k == 7)),
                 reads=[sqres(k), "ONES"], writes=[pr])
        S.op("act", lambda e: e.activation(out=rsbuf[:, 0:Tn], in_=pb[:, 0:Tn], func=AF.Ln,
                                           bias=EPS[:, 0:1], scale=1.0 / DM),
             reads=[pr, "EPS"], writes=[rsres])
        S.op("act", lambda e: e.activation(out=rsbuf[:, 0:Tn], in_=rsbuf[:, 0:Tn], func=AF.Exp, scale=-0.5),
             reads=[rsres], writes=[rsres])

    def stage_A(l, xc0, Tn, part="all"):
        xr = ("X", xc0)
        for hh in range(2 if part in ("all", "sq") else 0):
            S.op("act", lambda e, hh=hh: e.activation(out=SQ[:, 4 * hh:4 * hh + 4, 0:Tn],
                                                      in_=X[:, 4 * hh:4 * hh + 4, xc0:xc0 + Tn], func=AF.Square),
                 reads=[xr], writes=[("SQ", hh)])
        if part == "sq":
            return
        rms_stage(None, SQ, lambda k: ("SQ", k // 4), RS, "RS", Tn)
        for k in range(8):
            S.op("dve", lambda e, k=k: e.scalar_tensor_tensor(out=H[:, k, 0:Tn], in0=X[:, k, xc0:xc0 + Tn],
                                                              scalar=GAINS[:, 0, l, k:k + 1], in1=RS[:, 0:Tn],
                                                              op0=ALU.mult, op1=ALU.mult),
                 reads=[xr, "RS", "GAINS"], writes=[("H", k)])

    def proj_fm(m, bank, bres, off, Tn):
        for k in range(8):
            S.op("pe", lambda e, k=k: e.matmul(bank[:, off:off + Tn], lhsT=WI[:, k, m * 128:(m + 1) * 128],
                                                rhs=H[:, k, 0:Tn], start=(k == 0), stop=(k == 7)),
                 reads=[("WI", wi_piece(m * 128)), ("H", k)], writes=[bres])

    def proj_tm(col0, ncols, bank, bres, off, tok0):
        for k in range(8):
            S.op("pe", lambda e, k=k: e.matmul(bank[:, off:off + ncols], lhsT=H[:, k, tok0:tok0 + 128],
                                                rhs=WI[:, k, col0:col0 + ncols], start=(k == 0), stop=(k == 7)),
                 reads=[("WI", wi_piece(col0)), ("H", k)], writes=[bres])

    def pair_fm(m0, Tn):
        bank, bres = newbank()
        proj_fm(m0, bank, bres, 0, Tn)
        proj_fm(m0 + 1, bank, bres, Tn, Tn)
        return bank, bres

    def stage_B1(l, Tn, slots, mode, sample=False, kvout=None, part="all"):
        nb = Tn // 128
        full = (mode == "full")
        if part == "b":
            if full:
                bank, bres = pair_fm(10, Tn)
                S.op("dve", lambda e: e.tensor_copy(out=QT[:, 2:4, 0:Tn], in_=v3(bank[:, 0:2 * Tn], 2)),
                     reads=[bres], writes=["QT"])
                bank, bres = pair_fm(18, Tn)
                S.op("dve", lambda e: e.tensor_copy(out=MQ[:, :, 0:Tn], in_=v3(bank[:, 0:2 * Tn], 2)),
                     reads=[bres], writes=["MQ"])
            return
        bank, bres = newbank()
        proj_fm(12, bank, bres, 0, Tn)
        for b in range(nb):
            proj_tm(1664, 128, bank, bres, 256 + b * 128, b * 128)
        for b in range(nb):
            sl = slots[b]
            S.op("dve", lambda e: e.tensor_copy(out=KT[:, sl, :], in_=bank[:, b * 128:(b + 1) * 128]),
                 reads=[bres], writes=[("KT", sl)])
            S.op("act", lambda e: e.activation(out=VT[:, sl, 0, 0:64],
                                               in_=bank[:, 256 + b * 128:256 + b * 128 + 64], func=AF.Copy),
                 reads=[bres], writes=[("VT", sl)])
            S.op("act", lambda e: e.activation(out=VT[:, sl, 1, 64:128],
                                               in_=bank[:, 256 + b * 128 + 64:256 + (b + 1) * 128], func=AF.Copy),
                 reads=[bres], writes=[("VT", sl)])
        if kvout is not None:
            bank, bres = newbank()
            tok0 = Tn - 128
            proj_tm(1536, 256, bank, bres, 0, tok0)
            S.op("act", lambda e: e.activation(out=RC[:, 0:256], in_=bank[:, 0:256], func=AF.Copy),
                 reads=[bres], writes=[("RC", 0), ("RC", 1)])
            kvout()
        if full:
            for jj in range(2 if part == "all" else 1):
                bank, bres = pair_fm(8 + 2 * jj, Tn)
                S.op("dve", lambda e: e.tensor_copy(out=QT[:, 2 * jj:2 * jj + 2, 0:Tn], in_=v3(bank[:, 0:2 * Tn], 2)),
                     reads=[bres], writes=["QT"])
            if part == "all":
                bank, bres = pair_fm(18, Tn)
                S.op("dve", lambda e: e.tensor_copy(out=MQ[:, :, 0:Tn], in_=v3(bank[:, 0:2 * Tn], 2)),
                     reads=[bres], writes=["MQ"])

    def stage_B2(l, Tn, mode, sample=False):
        full = (mode == "full")
        if full:
            bank, bres = pair_fm(0, Tn)
            S.op("act", lambda e: e.activation(out=CB[:, :, 0:Tn], in_=v3(bank[:, 0:2 * Tn], 2), func=AF.Copy),
                 reads=[bres], writes=["CB"])
        bank, bres = pair_fm(2, Tn)
        S.op("act", lambda e: e.activation(out=SG[:, :, 0:Tn], in_=v3(bank[:, 0:2 * Tn], 2), func=AF.Copy),
             reads=[bres], writes=["SG"])
        bank, bres = pair_fm(4, Tn)
        if not sample:
            S.op("dve", lambda e: e.tensor_copy(out=U[:, :, 0:2], in_=U[:, :, T:T + 2]), reads=["U"], writes=["U"])
            S.op("dve", lambda e: e.tensor_tensor(out=U[:, :, 2:2 + Tn], in0=v3(bank[:, 0:2 * Tn], 2),
                                                  in1=SG[:, :, 0:Tn], op=ALU.mult),
                 reads=[bres, "SG"], writes=["U"])
            ub, ures = U, "U"
        else:
            S.op("dve", lambda e: e.tensor_tensor(out=US[:, :, 32:160], in0=v3(bank[:, 0:2 * Tn], 2),
                                                  in1=SG[:, :, 0:Tn], op=ALU.mult),
                 reads=[bres, "SG"], writes=["US"])
            ub, ures = US, "US"
        if not full:
            return
        bank, bres = pair_fm(6, Tn)
        S.op("act", lambda e: e.activation(out=SG[:, :, 0:Tn], in_=v3(bank[:, 0:2 * Tn], 2), func=AF.Silu),
             reads=[bres], writes=["SG"])
        sh = 16 if sample else 1
        for i in range(2):
            S.op("dve", lambda e: e.tensor_scalar(out=ACC[:, i, 0:Tn], in0=ub[:, i, 2 * sh:2 * sh + Tn],
                                                  scalar1=CW[:, l, i, 2:3], scalar2=None, op0=ALU.mult),
                 reads=[ures, "CW"], writes=["ACC"])
            S.op("dve", lambda e: e.scalar_tensor_tensor(out=ACC[:, i, 0:Tn], in0=ub[:, i, sh:sh + Tn],
                                                         scalar=CW[:, l, i, 1:2], in1=ACC[:, i, 0:Tn],
                                                         op0=ALU.mult, op1=ALU.add),
                 reads=[ures, "CW", "ACC"], writes=["ACC"])
            S.op("dve", lambda e: e.scalar_tensor_tensor(out=ACC[:, i, 0:Tn], in0=ub[:, i, 0:Tn],
                                                         scalar=CW[:, l, i, 0:1], in1=ACC[:, i, 0:Tn],
                                                         op0=ALU.mult, op1=ALU.add),
                 reads=[ures, "CW", "ACC"], writes=["ACC"])
        S.op("dve", lambda e: e.tensor_tensor(out=ACC[:, :, 0:Tn], in0=ACC[:, :, 0:Tn], in1=CB[:, :, 0:Tn], op=ALU.mult),
             reads=["ACC", "CB"], writes=["ACC"])
        S.op("dve", lambda e: e.tensor_tensor(out=MIX[:, 0:2, 0:Tn], in0=ACC[:, :, 0:Tn], in1=SG[:, :, 0:Tn], op=ALU.mult),
             reads=["ACC", "SG"], writes=[("MIX", 0)])
        for jj in range(2):
            bank, bres = pair_fm(14 + 2 * jj, Tn)
            S.op("act", lambda e: e.activation(out=SAG[:, 2 * jj:2 * jj + 2, 0:Tn], in_=v3(bank[:, 0:2 * Tn], 2), func=AF.Silu),
                 reads=[bres], writes=["SAG"])
        bank, bres = pair_fm(20, Tn)
        S.op("act", lambda e: e.activation(out=SMG[:, :, 0:Tn], in_=v3(bank[:, 0:2 * Tn], 2), func=AF.Silu),
             reads=[bres], writes=["SMG"])

    def rows(g):
        return (slice(0, 64), slice(64, 128)) if g == 0 else (slice(64, 128), slice(0, 64))

    def finish_heads(l, g, bank_o, bres, ncol, sink, gate, gres, mixk0, nh, tok0, tokn, gate_eng="dve"):
        orows, srows = rows(g)
        if sink:
            for j in range(nh):
                S.op("act", lambda e, j=j: e.activation(out=RC[srows, j * tokn:(j + 1) * tokn],
                                                        in_=bank_o[srows, j * tokn:(j + 1) * tokn], func=AF.Ln,
                                                        bias=ESINK[srows, l * 8 + 4 * g + j:l * 8 + 4 * g + j + 1]),
                     reads=[bres, "ESINK"], writes=[("RC", g)])
        else:
            S.op("act", lambda e: e.activation(out=RC[srows, 0:ncol], in_=bank_o[srows, 0:ncol], func=AF.Ln),
                 reads=[bres], writes=[("RC", g)])
        S.op("act", lambda e: e.activation(out=RC[srows, 0:ncol], in_=RC[srows, 0:ncol], func=AF.Exp, scale=-1.0),
             reads=[("RC", g)], writes=[("RC", g)])
        S.op("dve", lambda e: e.tensor_tensor(out=TMP[orows, 0:ncol], in0=bank_o[orows, 0:ncol], in1=RC[srows, 0:ncol], op=ALU.mult),
             reads=[bres, ("RC", g)], writes=[("TMP", g)])
        S.op(gate_eng, lambda e: e.tensor_tensor(out=MIX[orows, mixk0:mixk0 + nh, tok0:tok0 + tokn],
                                              in0=v3(TMP[orows, 0:ncol], nh),
                                              in1=gate[orows, 0:nh, tok0:tok0 + tokn], op=ALU.mult),
             reads=[("TMP", g), gres], writes=[("MIX", 1 + g)])

    def score_exp_mask(lhsT, rhs, lres, mask_idx):
        bank, bres = newbank()
        S.op("pe", lambda e: e.matmul(bank[:, 0:512], lhsT=lhsT, rhs=rhs, start=True, stop=True),
             reads=lres, writes=[bres])
        pt, pres = newpt()
        S.op("act", lambda e: e.activation(out=pt[:, 0:512], in_=bank[:, 0:512], func=AF.Exp, scale=0.125),
             reads=[bres], writes=[pres])
        if mask_idx is not None:
            S.op("dve", lambda e: e.tensor_tensor(out=v3(pt[:, 0:512], 4), in0=v3(pt[:, 0:512], 4),
                                                  in1=MASKS[:, mask_idx, :].unsqueeze(1).broadcast_to([128, 4, 128]),
                                                  op=ALU.mult),
                 reads=[pres, "MASKS"], writes=[pres])
        return pt, pres

    def stage_C_prompt(l, c, slots):
        nb = T // 128

        def win_scores(b):
            sl_prev = slots[b - 1] if b > 0 else (slots[0] - 1) % 4
            sl_own = slots[b]
            first = (c == NHALO and b == 0)
            tiles = {}
            for kb, sl in ((0, sl_prev), (1, sl_own)):
                for g in range(2):
                    gs = slice(64 * g, 64 * g + 64)
                    bank, bres = newbank()
                    S.op("pe", lambda e: e.matmul(bank[:, 0:512], lhsT=KT[gs, sl, :], rhs=QT[gs, 0:4, b * 128:(b + 1) * 128],
                                                  start=True, stop=True),
                         reads=[("KT", sl), "QT"], writes=[bres])
                    tiles[(kb, g)] = (bank, bres, sl)
            out = {0: [], 1: []}
            for kb in range(2):
                for g in range(2):
                    bank, bres, sl = tiles[(kb, g)]
                    pt, pres = newpt()
                    S.op("act", lambda e: e.activation(out=pt[:, 0:512], in_=bank[:, 0:512], func=AF.Exp, scale=0.125),
                         reads=[bres], writes=[pres])
                    midx = 0 if kb == 1 else (2 if first else 1)
                    S.op("dve", lambda e: e.tensor_tensor(out=v3(pt[:, 0:512], 4), in0=v3(pt[:, 0:512], 4),
                                                          in1=MASKS[:, midx, :].unsqueeze(1).broadcast_to([128, 4, 128]),
                                                          op=ALU.mult),
                         reads=[pres, "MASKS"], writes=[pres])
                    out[g].append((pt, pres, sl))
            return out

        def win_pv(b, g, pts):
            bank_o, bres = newbank()
            for n, (pt, pres, sl) in enumerate(pts):
                S.op("pe", lambda e: e.matmul(bank_o[:, 0:512], lhsT=VT[:, sl, g, :], rhs=pt[:, 0:512],
                                              start=(n == 0), stop=(n == 1)),
                     reads=[("VT", sl), pres], writes=[bres])
            finish_heads(l, g, bank_o, bres, 512, True, SAG, "SAG", 2, 4, b * 128, 128)

        def mem_scores(r):
            rs_ = slice(64 * r, 64 * r + 64)
            pts = []
            for mb in range(2):
                bank, bres = newbank()
                for i in range(2):
                    S.op("pe", lambda e: e.matmul(bank[:, i * T:(i + 1) * T], lhsT=MKT[rs_, i, mb * 128:(mb + 1) * 128],
                                                  rhs=MQ[rs_, i, 0:T], start=True, stop=True),
                         reads=["MKT", "MQ"], writes=[bres])
                pt, pres = newpt()
                S.op("act", lambda e: e.activation(out=pt[:, 0:512], in_=bank[:, 0:512], func=AF.Exp, scale=0.125),
                     reads=[bres], writes=[pres])
                pts.append((pt, pres))
            return pts

        def mem_pv(r, pts):
            bank_o, bres = newbank()
            for i in range(2):
                for mb in range(2):
                    pt, pres = pts[mb]
                    S.op("pe", lambda e: e.matmul(bank_o[:, i * T:(i + 1) * T], lhsT=MV[:, mb, 2 * i + r, :],
                                                  rhs=pt[:, i * T:(i + 1) * T], start=(mb == 0), stop=(mb == 1)),
                         reads=["MV", pres], writes=[bres])
            finish_heads(l, r, bank_o, bres, 2 * T, False, SMG, "SMG", 6, 2, 0, T)

        w0 = win_scores(0)
        m0 = mem_scores(0)

        def rest():
            win_pv(0, 0, w0[0])
            win_pv(0, 1, w0[1])
            mem_pv(0, m0)
            w1 = win_scores(1)
            m1 = mem_scores(1)
            win_pv(1, 0, w1[0])
            win_pv(1, 1, w1[1])
            mem_pv(1, m1)
        return rest

    def stage_D(l, xc0, Tn):
        xr = ("X", xc0)
        mixres = [("MIX", 0), ("MIX", 1), ("MIX", 2)]
        for n2 in range(4):
            bank, bres = newbank()
            for q in range(2):
                n = 2 * n2 + q
                for k in range(8):
                    S.op("pe", lambda e, k=k, n=n, q=q, bank=bank: e.matmul(bank[:, q * Tn:(q + 1) * Tn],
                                                                            lhsT=WO[:, k, n * 128:(n + 1) * 128],
                                                                            rhs=MIX[:, k, 0:Tn], start=(k == 0), stop=(k == 7)),
                         reads=["WO"] + mixres, writes=[bres])
            S.op("act", lambda e, bank=bank, n2=n2: e.activation(out=Y[:, 2 * n2:2 * n2 + 2, 0:Tn], in_=v3(bank[:, 0:2 * Tn], 2), func=AF.Copy),
                 reads=[bres], writes=[("Y", n2)])
            S.op("act", lambda e, bank=bank, n2=n2: e.activation(out=YSQ[:, 2 * n2:2 * n2 + 2, 0:Tn], in_=v3(bank[:, 0:2 * Tn], 2), func=AF.Square),
                 reads=[bres], writes=[("SQ", n2 // 2)])
        rms_stage(None, YSQ, lambda k: ("SQ", k // 4), RS2, "RS2", Tn)
        for n in range(8):
            S.op("dve", lambda e, n=n: e.scalar_tensor_tensor(out=Y[:, n, 0:Tn], in0=Y[:, n, 0:Tn],
                                                              scalar=GAINS[:, 1, l, n:n + 1], in1=RS2[:, 0:Tn],
                                                              op0=ALU.mult, op1=ALU.mult),
                 reads=[("Y", n // 2), "RS2", "GAINS"], writes=[("Y", n // 2)])
        S.op("dve", lambda e: e.tensor_tensor(out=X[:, :, xc0:xc0 + Tn], in0=X[:, :, xc0:xc0 + Tn], in1=Y[:, :, 0:Tn], op=ALU.add),
             reads=[xr] + [("Y", i) for i in range(4)], writes=[xr])

    def stage_mem(l):
        yres = [("Y", i) for i in range(4)]
        S.dma("sp", lambda e: e.dma_start(out=Y[:, :, :], in_=mem_T.rearrange("(k p) t -> p k t", p=128)),
              writes=yres, key="MEMT")
        for k in range(8):
            S.dma("pool", lambda e, k=k: e.dma_start(out=WO[:, k // 2, (k % 2) * 512:(k % 2 + 1) * 512],
                                                     in_=w_mem[l, k * 128:(k + 1) * 128, :]),
                  writes=["WO"], key=("WM", k))
        import os
        MS = int(os.environ.get("MK_MEMSTEP", "99"))
        if MS < 2:
            return
        for hh in range(2):
            S.op("act", lambda e, hh=hh: e.activation(out=SQ[:, 4 * hh:4 * hh + 4, :], in_=Y[:, 4 * hh:4 * hh + 4, :], func=AF.Square),
                 reads=yres, writes=[("SQ", hh)])
        if MS < 3:
            return
        rms_stage(None, SQ, lambda k: ("SQ", k // 4), RS, "RS", 256)
        if MS < 4:
            return
        for k in range(8):
            S.op("dve", lambda e, k=k: e.scalar_tensor_tensor(out=H[:, k, :], in0=Y[:, k, :], scalar=GAINS[:, 2, l, k:k + 1],
                                                              in1=RS[:, :], op0=ALU.mult, op1=ALU.mult),
                 reads=yres + ["RS", "GAINS"], writes=[("H", k)])

        def wm(k):
            return WO[:, k // 2, (k % 2) * 512:(k % 2 + 1) * 512]
        if MS < 5:
            return
        for mb in range(2):
            bank, bres = newbank()
            for k in range(8):
                S.op("pe", lambda e, k=k, bank=bank: e.matmul(bank[:, 0:512], lhsT=H[:, k, mb * 128:(mb + 1) * 128], rhs=wm(k),
                                                               start=(k == 0), stop=(k == 7)),
                     reads=["WO", ("H", k)], writes=[bres])
            accf = ACC[:].rearrange("p a b -> p (a b)")
            S.op("act", lambda e, bank=bank: e.activation(out=accf, in_=bank[:, 0:512], func=AF.Copy), reads=[bres], writes=["ACC"])
            if MS < 6:
                continue
            bv = bank[:, 256:512].rearrange("p (i r d) -> p i r d", i=2, r=2)
            S.op("dve", lambda e: e.tensor_copy(out=MV[:, mb, :, 0:64].rearrange("p (i r) d -> p i r d", r=2)[:, :, 0, :], in_=bv[:, :, 0, :]),
                 reads=[bres], writes=["MV"])
            S.op("dve", lambda e: e.tensor_copy(out=MV[:, mb, :, 64:128].rearrange("p (i r) d -> p i r d", r=2)[:, :, 1, :], in_=bv[:, :, 1, :]),
                 reads=[bres], writes=["MV"])
            S.dma("sp", lambda e: e.dma_start(out=mk_p[l, mb * 128:(mb + 1) * 128, :], in_=accf[:, 0:256]), reads=["ACC"], key="mkvo", final=True)
            S.dma("sp", lambda e: e.dma_start(out=mv_p[l, mb * 128:(mb + 1) * 128, :], in_=accf[:, 256:512]), reads=["ACC"], key="mkvo", final=True)
        if MS < 7:
            return
        bank, bres = newbank()
        for i in range(2):
            for k in range(8):
                lw = wm(k)[:, 128 * i:128 * i + 128]
                S.op("pe", lambda e, k=k, i=i, lw=lw: e.matmul(bank[:, i * 256:(i + 1) * 256], lhsT=lw, rhs=H[:, k, :],
                                                               start=(k == 0), stop=(k == 7)),
                     reads=["WO", ("H", k)], writes=[bres])
        S.op("dve", lambda e: e.tensor_copy(out=MKT[:, :, :], in_=v3(bank[:, 0:512], 2)), reads=[bres], writes=["MKT"])

    def stage_C_sample(l, slot, next_wi):
        ring["list"] = list(range(5))
        ring["pos"] = 0
        ptring["n"] = NPT
        ptpos[0] = 0
        bog = [(banks[5], ("PS", 5)), (banks[6], ("PS", 6))]
        bom, bomres = banks[7], ("PS", 7)
        for g in range(2):
            gs = slice(64 * g, 64 * g + 64)
            pt, pres = score_exp_mask(KT[gs, slot, :], QT[gs, 0:4, 0:128], [("KT", slot), "QT"], 3)
            bo, bor = bog[g]
            S.op("pe", lambda e, bo=bo, pt=pt: e.matmul(bo[:, 0:512], lhsT=VT[:, slot, g, :], rhs=pt[:, 0:512], start=True, stop=False),
                 reads=[("VT", slot), pres], writes=[bor])
        def k_dma(s):
            par = s % 2
            S.dma("sp", lambda e: e.dma_start(out=KCS[par][:], in_=ckw[l, s]), writes=[("KCS", par)], key=("kc", par))
            S.dma("sp", lambda e: e.dma_start(out=MKS[par][:], in_=cmk[l, s].rearrange("(p b) c -> p b c", b=2)),
                  writes=[("MKS", par)], key=("mk", par))

        def v_dma(s):
            par = s % 2
            S.dma("pool", lambda e: e.dma_start(out=VC[par][:], in_=cvw[l, s]), writes=[("VC", par)], key=("vc", par))
            S.dma("pool", lambda e: e.dma_start(out=MVS[par][:], in_=cmv[l, s].rearrange("(p b) c -> p b c", b=2)),
                  writes=[("MVS", par)], key=("mv", par))

        def tr(s):
            par = s % 2
            ba, bares = newbank()
            bb, bbres = newbank()
            S.op("pe", lambda e: e.transpose(out=ba[:, 0:128], in_=KCS[par][:, :], identity=IDF[:, :]),
                 reads=[("KCS", par), "IDF"], writes=[bares])
            n = 0
            for i in range(2):
                for mb in range(2):
                    src = MKS[par][:, mb, 128 * i:128 * i + 128]
                    n += 1
                    if n < 4:
                        S.op("pe", lambda e: e.transpose(out=ba[:, n * 128:(n + 1) * 128], in_=src, identity=IDF[:, :]),
                             reads=[("MKS", par), "IDF"], writes=[bares])
                    else:
                        S.op("pe", lambda e: e.transpose(out=bb[:, 0:128], in_=src, identity=IDF[:, :]),
                             reads=[("MKS", par), "IDF"], writes=[bbres])
            mflat = MKTS[par][:].rearrange("p a b c -> p (a b c)")
            S.op("dve", lambda e: e.tensor_copy(out=KCT[par][:, :], in_=ba[:, 0:128]), reads=[bares], writes=[("KCT", par)])
            S.op("dve", lambda e: e.tensor_copy(out=mflat[:, 0:384], in_=ba[:, 128:512]), reads=[bares], writes=[("MKTS", par)])
            S.op("dve", lambda e: e.tensor_copy(out=mflat[:, 384:512], in_=bb[:, 0:128]), reads=[bbres], writes=[("MKTS", par)])

        def sc(s):
            par = s % 2
            bsb = [newbank(), newbank()]
            for g in range(2):
                gs = slice(64 * g, 64 * g + 64)
                bk, bkres = bsb[g]
                S.op("pe", lambda e: e.matmul(v3(bk[:, 0:32], 4), lhsT=KCT[par][gs, :],
                                              rhs=QT[gs, 0:4, s:128:16], start=True, stop=True),
                     reads=[("KCT", par), "QT"], writes=[bkres])
            for r in range(2):
                rs_ = slice(64 * r, 64 * r + 64)
                bk, bkres = bsb[r]
                for i in range(2):
                    for mb in range(2):
                        o0 = 32 + (i * 2 + mb) * 8
                        S.op("pe", lambda e: e.matmul(bk[:, o0:o0 + 8], lhsT=MKTS[par][rs_, i, mb, :],
                                                      rhs=MQ[rs_, i, s:128:16], start=True, stop=True),
                             reads=[("MKTS", par), "MQ"], writes=[bkres])
            return bsb

        def ex(s, bsb):
            par = s % 2
            for rg in range(2):
                bk, bkres = bsb[rg]
                S.op("act", lambda e: e.activation(out=PSS[par][:, 64 * rg:64 * rg + 64], in_=bk[:, 0:64], func=AF.Exp, scale=0.125),
                     reads=[bkres], writes=[("PSS", par)])
            pw = PSS[par][:, :].rearrange("p (g x) -> p g x", g=2)[:, :, 0:32].rearrange("p g (j t) -> p g j t", j=4)
            S.op("dve", lambda e: e.tensor_tensor(out=pw, in0=pw,
                                                  in1=MASKS[:, 4, 0:8].unsqueeze(1).unsqueeze(1).broadcast_to([128, 2, 4, 8]), op=ALU.mult),
                 reads=[("PSS", par), "MASKS"], writes=[("PSS", par)])

        def pv(s):
            par = s % 2
            for g in range(2):
                orows, srows = rows(g)
                bo, bor = bog[g]
                ov = bo[orows, 0:512].rearrange("p (j t s) -> p j t s", j=4, t=8)[:, :, :, s]
                sv = bo[srows, 0:512].rearrange("p (j t s) -> p j t s", j=4, t=8)[:, :, :, s]
                rh = v3(PSS[par][:, 64 * g:64 * g + 32], 4)
                S.op("pe", lambda e: e.matmul(ov, lhsT=VC[par][:, g * 64:(g + 1) * 64], rhs=rh, start=False, stop=False),
                     reads=[("VC", par), ("PSS", par)], writes=[bor])
                S.op("pe", lambda e: e.matmul(sv, lhsT=ONES[:, 0:64], rhs=rh, start=False, stop=(s == NSEQ - 1)),
                     reads=["ONES", ("PSS", par)], writes=[bor])
            for r in range(2):
                orows, srows = rows(r)
                for i in range(2):
                    h = 2 * i + r
                    c0 = (r * 2 + i) * 128
                    ov = bom[orows, c0:c0 + 128].rearrange("p (t s) -> p t s", t=8)[:, :, s]
                    sv = bom[srows, c0:c0 + 128].rearrange("p (t s) -> p t s", t=8)[:, :, s]
                    for mb in range(2):
                        o0 = 64 * r + 32 + (i * 2 + mb) * 8
                        S.op("pe", lambda e: e.matmul(ov, lhsT=MVS[par][:, mb, h * 64:(h + 1) * 64],
                                                      rhs=PSS[par][:, o0:o0 + 8], start=(mb == 0), stop=(mb == 1)),
                             reads=[("MVS", par), ("PSS", par)], writes=[bomres])
                        S.op("pe", lambda e: e.matmul(sv, lhsT=ONES[:, 0:64], rhs=PSS[par][:, o0:o0 + 8],
                                                      start=(mb == 0), stop=(mb == 1)),
                             reads=["ONES", ("PSS", par)], writes=[bomres])

        k_dma(0)
        k_dma(1)
        v_dma(0)
        tr(0)
        for s in range(NSEQ):
            if s + 2 < NSEQ:
                k_dma(s + 2)
            if s + 1 < NSEQ:
                tr(s + 1)
            bsb = sc(s)
            if s >= 1:
                pv(s - 1)
            if s + 1 < NSEQ:
                v_dma(s + 1)
            ex(s, bsb)
            if next_wi is not None and s % 2 == 1:
                next_wi(s // 2)
        pv(NSEQ - 1)
        for g in range(2):
            bo, bor = bog[g]
            finish_heads(l, g, bo, bor, 512, True, SAG, "SAG", 2, 4, 0, 128)
        for r in range(2):
            orows, srows = rows(r)
            S.op("act", lambda e, r=r, srows=srows: e.activation(out=RC[srows, 0:256], in_=bom[srows, r * 256:(r + 1) * 256], func=AF.Ln),
                 reads=[bomres], writes=[("RC", r)])
            S.op("act", lambda e, srows=srows: e.activation(out=RC[srows, 0:256], in_=RC[srows, 0:256], func=AF.Exp, scale=-1.0),
                 reads=[("RC", r)], writes=[("RC", r)])
            S.op("dve", lambda e, r=r, orows=orows, srows=srows: e.tensor_tensor(out=TMP[orows, 0:256], in0=bom[orows, r * 256:(r + 1) * 256],
                                                                                 in1=RC[srows, 0:256], op=ALU.mult),
                 reads=[bomres, ("RC", r)], writes=[("TMP", r)])
            S.op("dve", lambda e, orows=orows: e.tensor_tensor(out=MIX[orows, 6:8, 0:128], in0=v3(TMP[orows, 0:256], 2),
                                                               in1=SMG[orows, 0:2, 0:128], op=ALU.mult),
                 reads=[("TMP", r), "SMG"], writes=[("MIX", 1 + r)])
        ring["list"] = list(range(8))
        ring["pos"] = 0
        ptring["n"] = 8
        ptpos[0] = 0

    import os
    LIMIT = int(os.environ.get("MK_LIMIT", "99"))
    NCH = int(os.environ.get("MK_NCH", str(NPC)))

    def program():
        if LIMIT < 1:
            return
        SKIP = os.environ.get("MK_SKIP", "")
        for l in range(depth):
            if "mem" not in SKIP:
                stage_mem(l)
            if l == 0:
                load_rest_inputs()
                for pi_ in range(4):
                    for k in range(8):
                        load_wi_piece(0, pi_, k)
            if "wo" not in SKIP:
                load_wo(l)
            if LIMIT < 2:
                return
            last = (l == depth - 1)
            plan = []
            for c in range(NPC):
                if c >= NCH and c < NPC - 1:
                    continue
                if c < NHALO:
                    lv = depth - 1 - l
                    if c == 1:
                        mode = "full" if lv >= 1 else "kv"
                    else:
                        mode = "full" if lv >= 3 else ("kv" if lv == 2 else "skip")
                else:
                    mode = "full"
                if mode != "skip":
                    plan.append((c, mode))
            S.dma("sp", lambda e: e.dma_start(out=US[:, :, 0:32], in_=sconv[l].rearrange("(i p) c -> p i c", p=128)),
                  writes=["US"], key="sconv")

            def kvout_p(l=l):
                S.dma("sp", lambda e: e.dma_start(out=wk_p[l], in_=RC[:, 0:128]), reads=[("RC", 0), ("RC", 1)], key="kvo", final=True)
                S.dma("sp", lambda e: e.dma_start(out=wv_p[l], in_=RC[:, 128:256]), reads=[("RC", 0), ("RC", 1)], key="kvo", final=True)

            def kvout_s(l=l):
                for t in range(8):
                    S.dma("sp", lambda e: e.dma_start(out=wk_s[l, :, 120 + t, :], in_=RC[16 * t:16 * t + 16, 0:128]),
                          reads=[("RC", 0), ("RC", 1)], key="kvo", final=True)
                    S.dma("sp", lambda e: e.dma_start(out=wv_s[l, :, 120 + t, :], in_=RC[16 * t:16 * t + 16, 128:256]),
                          reads=[("RC", 0), ("RC", 1)], key="kvo", final=True)

            def slots_of(c):
                return [(2 * c + 1) % 4, (2 * c + 2) % 4]
            sslot = 1

            def b1_of(pi, part="all"):
                if pi < len(plan):
                    c, mode = plan[pi]
                    stage_B1(l, T, slots_of(c), mode, sample=False, kvout=(kvout_p if c == NPC - 1 else None), part=part)
                else:
                    stage_B1(l, TS, [sslot], "full", sample=True, kvout=kvout_s, part=part)

            stage_A(l, plan[0][0] * T, T)
            b1_of(0)
            for pi, (c, mode) in enumerate(plan):
                xc0 = c * T
                slots = slots_of(c)
                rest = None
                nxt = (plan[pi + 1][0] * T, T) if pi + 1 < len(plan) else (XS0, TS)
                stage_A(l, nxt[0], nxt[1], part="sq")
                if mode == "full":
                    rest = stage_C_prompt(l, c, slots)
                stage_B2(l, T, mode, sample=False)
                if c == NPC - 1:
                    S.dma("sp", lambda e: e.dma_start(out=conv_p[l].rearrange("(i p) j -> p i j", p=128), in_=U[:, :, T:T + 2]),
                          reads=["U"], key="cvo", final=True)
                stage_A(l, nxt[0], nxt[1], part="rest")
                if rest is not None:
                    rest()
                if mode == "full":
                    b1_of(pi + 1, "a")
                    stage_D(l, xc0, T)
                    b1_of(pi + 1, "b")
                else:
                    b1_of(pi + 1)
                if l == 0 and pi in (1, 4, 7) and (pi // 3 + 1) < depth:
                    precast(pi // 3 + 1, ("X", xc0))
            stage_B2(l, TS, "full", sample=True)
            S.dma("sp", lambda e: e.dma_start(out=conv_s[l].rearrange("(i p) c -> p i c", p=128), in_=US[:, :, 128:160]),
                  reads=["US"], key="cvso", final=True)
            stage_C_sample(l, sslot, None)
            if not last:
                for pi_ in range(4):
                    for k in range(8):
                        load_wi_piece(l + 1, pi_, k)
            if LIMIT < 7:
                return
            stage_D(l, XS0, TS)

    program()
    S.dma("sp", lambda e: e.dma_start(out=y_T.rearrange("(k p) t -> p k t", p=128), in_=X[:, :, NHALO * T:NPC * T]),
          reads=[("X", c * T) for c in range(NHALO, NPC)], key="yo", final=True)
    S.dma("sp", lambda e: e.dma_start(out=ys_T.rearrange("(k p) t -> p k t", p=128), in_=X[:, :, XS0:XS0 + TS]),
          reads=[("X", XS0)], key="yso", final=True)
    stats = S.finalize()
    return nc, stats


def _perms():
    pin = list(range(0, 1024))
    pin += [1024 + 64 * (4 * r + j) + d for j in range(4) for r in range(2) for d in range(64)]
    pin += list(range(1536, 1792))
    pin += [1792 + 64 * (4 * r + j) + d for j in range(4) for r in range(2) for d in range(64)]
    pin += list(range(2304, 2816))
    pout = list(range(0, 256))
    pout += [256 + 64 * (4 * r + j) + d for j in range(4) for r in range(2) for d in range(64)]
    pout += list(range(768, 1024))
    return np.array(pin), np.array(pout)


_NC_CACHE = {}


def kernel(x_prompt, x_sample, mem_prompt, cache_win_k, cache_win_v, state_conv,
           cache_mem_k, cache_mem_v, norm_pre, norm_post, norm_mem, w_in, conv_w,
           attn_sinks, w_mem_kv, w_out):
    f = np.float32
    x_prompt = np.asarray(x_prompt, f); x_sample = np.asarray(x_sample, f)
    mem_prompt = np.asarray(mem_prompt, f)
    cache_win_k = np.asarray(cache_win_k, f); cache_win_v = np.asarray(cache_win_v, f)
    state_conv = np.asarray(state_conv, f)
    cache_mem_k = np.asarray(cache_mem_k, f); cache_mem_v = np.asarray(cache_mem_v, f)
    norm_pre = np.asarray(norm_pre, f); norm_post = np.asarray(norm_post, f); norm_mem = np.asarray(norm_mem, f)
    w_in = np.asarray(w_in, f); conv_w = np.asarray(conv_w, f); attn_sinks = np.asarray(attn_sinks, f)
    w_mem_kv = np.asarray(w_mem_kv, f); w_out = np.asarray(w_out, f)

    pin, pout = _perms()
    w_in_p = np.ascontiguousarray(w_in[:, :, pin])
    w_out_p = np.ascontiguousarray(w_out[:, pout, :])
    w_mem_c = np.ascontiguousarray(w_mem_kv)
    gains = np.stack([norm_pre, norm_post, norm_mem], 0).reshape(3, DEPTH, 8, 128)
    gains = np.ascontiguousarray(gains.transpose(3, 0, 1, 2).reshape(128, 3 * DEPTH * 8))
    cw = conv_w.reshape(DEPTH, 3, 2, 128)
    cw = np.ascontiguousarray(cw.transpose(3, 0, 2, 1).reshape(128, DEPTH * 6))
    sinks = np.ascontiguousarray(np.broadcast_to(attn_sinks.reshape(1, DEPTH * 8), (128, DEPTH * 8))).astype(f)
    ident = np.eye(128, dtype=f)
    tk = np.arange(128)[:, None]; tq = np.arange(128)[None, :]
    m_own = (tk <= tq).astype(f)
    m_prev = (tk > tq).astype(f)
    m_snew = (((tk % 16) == (tq % 16)) & ((tk // 16) <= (tq // 16))).astype(f)
    m_sc = np.zeros((128, 128), f)
    m_sc[:, 0:8] = (np.arange(128)[:, None] > np.arange(8)[None, :]).astype(f)

    in_maps = []
    for c in range(NCORES):
        b, q = c // 4, c % 4
        own = x_prompt[b, q * 2048:(q + 1) * 2048]
        halo = x_prompt[b, q * 2048 - 512:q * 2048] if q > 0 else np.zeros((512, DM), f)
        xp_T = np.ascontiguousarray(np.concatenate([halo, own], 0).T)
        xs = x_sample[16 * c:16 * c + 16]
        xs_T = np.ascontiguousarray(xs.transpose(2, 1, 0).reshape(DM, TS))
        m_pf = m_prev if q > 0 else np.zeros((128, 128), f)
        masks = np.ascontiguousarray(np.stack([m_own, m_prev, m_pf, m_snew, m_sc], 1).reshape(128, 5 * 128))
        sc = state_conv[:, 16 * c:16 * c + 16]
        sc = np.ascontiguousarray(sc.transpose(0, 3, 2, 1).reshape(DEPTH, 256, 32))
        in_maps.append({
            "xp_T": xp_T, "xs_T": xs_T,
            "mem_T": np.ascontiguousarray(mem_prompt[b].T),
            "ckw": np.ascontiguousarray(cache_win_k[:, 16 * c:16 * c + 16].reshape(DEPTH, 16, 128, 128)),
            "cvw": np.ascontiguousarray(cache_win_v[:, 16 * c:16 * c + 16].reshape(DEPTH, 16, 128, 128)),
            "cmk": np.ascontiguousarray(cache_mem_k[:, 16 * c:16 * c + 16].reshape(DEPTH, 16, 256, 256)),
            "cmv": np.ascontiguousarray(cache_mem_v[:, 16 * c:16 * c + 16].reshape(DEPTH, 16, 256, 256)),
            "sconv": sc, "gains": gains, "convw": cw, "sinks": sinks,
            "w_in": w_in_p, "w_out": w_out_p, "w_mem": w_mem_c,
            "masks": masks, "ident": ident,
        })

    if "nc" not in _NC_CACHE:
        import os
        _NC_CACHE["nc"] = build_nc(depth=int(os.environ.get("MK_DEPTH", str(DEPTH))))[0]
    nc = _NC_CACHE["nc"]
    res = run_bass_kernel_spmd(nc, in_maps, core_ids=list(range(NCORES)))
    R = res.results

    y_prompt = np.empty((2, 8192, DM), f)
    y_sample = np.empty((128, 8, DM), f)
    wkp = np.empty((DEPTH, 2, 128, 2, 64), f); wvp = np.empty_like(wkp)
    cvp = np.empty((DEPTH, 2, 2, 256), f)
    mkp = np.empty((DEPTH, 2, 256, 4, 64), f); mvp = np.empty_like(mkp)
    wks = np.empty((DEPTH, 128, 128, 2, 64), f); wvs = np.empty_like(wks)
    cvs = np.empty((DEPTH, 128, 2, 256), f)
    for c in range(NCORES):
        b, q = c // 4, c % 4
        r = R[c]
        y_prompt[b, q * 2048:(q + 1) * 2048] = r["y_T"].T
        y_sample[16 * c:16 * c + 16] = r["ys_T"].reshape(DM, 8, 16).transpose(2, 1, 0)
        if q == 3:
            wkp[:, b] = r["wk_p"].reshape(DEPTH, 128, 2, 64)
            wvp[:, b] = r["wv_p"].reshape(DEPTH, 128, 2, 64)
            cvp[:, b] = r["conv_p"].transpose(0, 2, 1)
        if q == 0:
            mkp[:, b] = r["mk_p"].reshape(DEPTH, 256, 4, 64)
            mvp[:, b] = r["mv_p"].reshape(DEPTH, 256, 4, 64)
        wks[:, 16 * c:16 * c + 16] = r["wk_s"].reshape(DEPTH, 16, 128, 2, 64)
        wvs[:, 16 * c:16 * c + 16] = r["wv_s"].reshape(DEPTH, 16, 128, 2, 64)
        cvs[:, 16 * c:16 * c + 16] = r["conv_s"].reshape(DEPTH, 256, 2, 16).transpose(0, 3, 2, 1)
    return (y_prompt, y_sample, wkp, wvp, cvp, mkp, mvp, wks, wvs, cvs)
```

```python
import numpy as np
import concourse.bass as bass
import concourse.mybir as mybir
from concourse.bass_utils import run_bass_kernel_spmd

F32 = mybir.dt.float32
BF16 = mybir.dt.bfloat16
AF = mybir.ActivationFunctionType
ALU = mybir.AluOpType

NCORES = 8
DEPTH = 4
DM = 1024
IN_DIM = 2816
T = 256
NHALO = 2
NOWN = 8
NPC = NHALO + NOWN
XS0 = NPC * T
TS = 128
XCOLS = XS0 + TS
NSEQ = 16


class _Rec:
    def __getattr__(self, name):
        def f(*args, **kw):
            self.call = (name, args, kw)
            return self
        return f


class Sched:
    ENGS = ("pe", "act", "dve", "pool", "sp")

    def __init__(self, nc):
        self.nc = nc
        self.eng_obj = {"pe": nc.tensor, "act": nc.scalar, "dve": nc.vector,
                        "pool": nc.gpsimd, "sp": nc.sync}
        self.instrs = []
        self.last_write = {}
        self.reads_since = {}
        self.dma_sem_count = {}
        self.dma_group = {}
        self.final_dma = []

    def _add(self, eng, emit, reads, writes, dma_key=None):
        i = len(self.instrs)
        rp = _Rec()
        emit(rp)
        emit = rp.call
        deps = set()
        for r in reads:
            deps.update(self.last_write.get(r, ()))
        par_dma = {}
        for r in writes:
            lw = self.last_write.get(r, [])
            rs = self.reads_since.get(r, [])
            if dma_key is not None and lw and not rs and all(self.instrs[w]["dma_key"] is not None for w in lw):
                par_dma[r] = True
                continue
            deps.update(lw)
            deps.update(rs)
        rec = dict(eng=eng, emit=emit, deps=deps, dma_key=dma_key, dma_val=None,
                   need_inc=False, tick=None)
        if dma_key is not None:
            c = self.dma_sem_count.get(dma_key, 0) + 1
            self.dma_sem_count[dma_key] = c
            rec["dma_val"] = 16 * c
            self.dma_group.setdefault(dma_key, []).append(i)
        self.instrs.append(rec)
        for r in reads:
            self.reads_since.setdefault(r, []).append(i)
        for r in writes:
            if r in par_dma:
                self.last_write[r].append(i)
            else:
                self.last_write[r] = [i]
                self.reads_since[r] = []
        return i

    def op(self, eng, emit, reads=(), writes=()):
        writes = tuple(writes) + tuple(r for r in reads if isinstance(r, tuple) and r[0] == "PS" and r not in writes)
        return self._add(eng, emit, tuple(reads), writes)

    def dma(self, queue, emit, reads=(), writes=(), key=None, final=False):
        i = self._add(queue, emit, tuple(reads), tuple(writes), dma_key=key)
        if final:
            self.final_dma.append(i)
        return i

    def seal(self, key):
        tot = 16 * self.dma_sem_count.get(key, 0)
        for i in self.dma_group.get(key, []):
            self.instrs[i]["dma_val"] = tot
        self.dma_group[key] = []

    def finalize(self):
        nc = self.nc
        ins = self.instrs
        for rec in ins:
            for d in rec["deps"]:
                p = ins[d]
                if p["dma_key"] is None and p["eng"] != rec["eng"]:
                    p["need_inc"] = True
        cnt = {e: 0 for e in self.ENGS}
        for rec in ins:
            if rec["dma_key"] is None and rec["need_inc"]:
                cnt[rec["eng"]] += 1
                rec["tick"] = cnt[rec["eng"]]
        eng_sem = {e: nc.alloc_semaphore(name=f"s_{e}") for e in self.ENGS if cnt[e] > 0}
        dma_sem = {k: nc.alloc_semaphore(name=f"d_{i}") for i, k in enumerate(self.dma_sem_count)}
        per_eng = {e: [] for e in self.ENGS}
        for idx, rec in enumerate(ins):
            per_eng[rec["eng"]].append(idx)

        def emit_engine(e):
            eo = self.eng_obj[e]
            waited = {}
            for idx in per_eng[e]:
                rec = ins[idx]
                need = {}
                for d in rec["deps"]:
                    p = ins[d]
                    if p["dma_key"] is not None:
                        k = ("d", p["dma_key"])
                        need[k] = max(need.get(k, 0), p["dma_val"])
                    elif p["eng"] != e:
                        k = ("c", p["eng"])
                        need[k] = max(need.get(k, 0), p["tick"])
                for k, v in need.items():
                    if waited.get(k, 0) >= v:
                        continue
                    waited[k] = v
                    sem = dma_sem[k[1]] if k[0] == "d" else eng_sem[k[1]]
                    eo.wait_ge(sem, v)
                fname, fargs, kw = rec["emit"]
                r = getattr(eo, fname)(*fargs, **kw)
                if rec["dma_key"] is not None:
                    r.then_inc(dma_sem[rec["dma_key"]], 16)
                elif rec["need_inc"]:
                    r.then_inc(eng_sem[e], 1)
            if e == "sp":
                for k, c in self.dma_sem_count.items():
                    eo.wait_ge(dma_sem[k], 16 * c)

        with nc.Block() as block:
            @block.tensor
            def _(x):
                emit_engine("pe")

            @block.scalar
            def _(x):
                emit_engine("act")

            @block.vector
            def _(x):
                emit_engine("dve")

            @block.gpsimd
            def _(x):
                emit_engine("pool")

            @block.sync
            def _(x):
                emit_engine("sp")
        return {e: len(per_eng[e]) for e in self.ENGS}


def build_nc(depth=DEPTH):
    nc = bass.Bass("TRN2", target_bir_lowering=False)

    def din(name, shape):
        return nc.dram_tensor(name, list(shape), F32, kind="ExternalInput").ap()

    def dout(name, shape):
        return nc.dram_tensor(name, list(shape), F32, kind="ExternalOutput").ap()

    xp_T = din("xp_T", [DM, NPC * T])
    xs_T = din("xs_T", [DM, TS])
    mem_T = din("mem_T", [DM, 256])
    ckw = din("ckw", [DEPTH, NSEQ, 128, 128])
    cvw = din("cvw", [DEPTH, NSEQ, 128, 128])
    cmk = din("cmk", [DEPTH, NSEQ, 256, 256])
    cmv = din("cmv", [DEPTH, NSEQ, 256, 256])
    sconv = din("sconv", [DEPTH, 256, 32])
    gains = din("gains", [128, 3 * DEPTH * 8])
    convw = din("convw", [128, DEPTH * 6])
    sinks = din("sinks", [128, DEPTH * 8])
    w_in = din("w_in", [DEPTH, DM, IN_DIM])
    w_out = din("w_out", [DEPTH, DM, DM])
    w_mem = din("w_mem", [DEPTH, DM, 512])
    masks = din("masks", [128, 5 * 128])
    ident = din("ident", [128, 128])

    y_T = dout("y_T", [DM, NOWN * T])
    ys_T = dout("ys_T", [DM, TS])
    wk_p = dout("wk_p", [DEPTH, 128, 128])
    wv_p = dout("wv_p", [DEPTH, 128, 128])
    conv_p = dout("conv_p", [DEPTH, 256, 2])
    mk_p = dout("mk_p", [DEPTH, 256, 256])
    mv_p = dout("mv_p", [DEPTH, 256, 256])
    wk_s = dout("wk_s", [DEPTH, NSEQ, 128, 128])
    wv_s = dout("wv_s", [DEPTH, NSEQ, 128, 128])
    conv_s = dout("conv_s", [DEPTH, 256, 32])

    win_bf = nc.dram_tensor("win_bf", [DEPTH - 1, DM, IN_DIM], BF16, kind="Internal").ap()
    wout_bf = nc.dram_tensor("wout_bf", [DEPTH - 1, DM, DM], BF16, kind="Internal").ap()

    S = Sched(nc)

    def sb(name, shape, dt):
        return nc.alloc_sbuf_tensor(name, list(shape), dt)

    X = sb("X", [128, 8, XCOLS], F32)
    WI = sb("WI", [128, 8, IN_DIM], BF16)
    WO = sb("WO", [128, 8, DM], BF16)
    SQ = sb("SQ", [128, 8, T], BF16)
    H = sb("H", [128, 8, T], BF16)
    Y = sb("Y", [128, 8, T], F32)
    YSQ = SQ
    RS = sb("RS", [128, T], F32)
    RS2 = sb("RS2", [128, T], F32)
    U = sb("U", [128, 2, T + 2], F32)
    US = sb("US", [128, 2, 160], F32)
    CB = sb("CB", [128, 2, T], BF16)
    SG = sb("SG", [128, 2, T], BF16)
    ACC = sb("ACC", [128, 2, T], F32)
    QT = sb("QT", [128, 4, T], BF16)
    KT = sb("KT", [128, 4, 128], BF16)
    VT = sb("VT", [128, 4, 2, 128], BF16)
    SAG = sb("SAG", [128, 4, T], BF16)
    SMG = sb("SMG", [128, 2, T], BF16)
    MQ = sb("MQ", [128, 2, T], BF16)
    MIX = sb("MIX", [128, 8, T], BF16)
    NPT = 4
    PT = [sb(f"PT{i}", [128, 512], BF16) for i in range(NPT)]
    RC = sb("RC", [128, 512], F32)
    TMP = sb("TMP", [128, 512], BF16)
    MKT = sb("MKT", [128, 2, 256], BF16)
    MV = sb("MV", [128, 2, 4, 128], BF16)
    ONES = sb("ONES", [128, 128], BF16)
    IDB = sb("IDB", [128, 128], BF16)
    MASKS = sb("MASKS", [128, 5, 128], BF16)
    GAINS = sb("GAINS", [128, 3, DEPTH, 8], F32)
    CW = sb("CW", [128, DEPTH, 2, 3], F32)
    ESINK = sb("ESINK", [128, DEPTH * 8], F32)
    EPS = sb("EPS", [128, 1], F32)
    KCS = [sb(f"KCS{i}", [128, 128], F32) for i in range(2)]
    IDF = sb("IDF", [128, 128], F32)
    KCT = [sb(f"KCT{i}", [128, 128], BF16) for i in range(2)]
    VC = [sb(f"VC{i}", [128, 128], BF16) for i in range(2)]
    MKS = [sb(f"MKS{i}", [128, 2, 256], F32) for i in range(2)]
    MKTS = [sb(f"MKTS{i}", [128, 2, 2, 128], BF16) for i in range(2)]
    MVS = [sb(f"MVS{i}", [128, 2, 256], BF16) for i in range(2)]
    PSS = [sb(f"PSS{i}", [128, 128], BF16) for i in range(2)]

    banks = [nc.alloc_psum_tensor(f"ps{i}", [128, 512], F32) for i in range(8)]
    ring = {"list": list(range(8)), "pos": 0}

    def newbank():
        lst = ring["list"]
        i = lst[ring["pos"] % len(lst)]
        ring["pos"] += 1
        return banks[i], ("PS", i)

    ptpos = [0]
    ptring = {"n": 8}
    PTX = [(PT[i][:, :], ("PT", i)) for i in range(NPT)]
    PTX += [(MKS[i][:].rearrange("p a b -> p (a b)").bitcast(BF16)[:, 0:512], ("MKS", i)) for i in range(2)]
    PTX += [(MKTS[i][:].rearrange("p a b c -> p (a b c)"), ("MKTS", i)) for i in range(2)]

    def newpt():
        i = ptpos[0] % ptring["n"]
        ptpos[0] += 1
        return PTX[i]

    def v3(ap2d, a):
        return ap2d.rearrange("p (a b) -> p a b", a=a)

    S.op("dve", lambda e: e.memset(ONES[:], 1.0), writes=["ONES"])
    S.op("dve", lambda e: e.memset(EPS[:], 1e-6), writes=["EPS"])
    S.op("dve", lambda e: e.memset(KT[:], 0.0), writes=[("KT", i) for i in range(4)])
    S.op("dve", lambda e: e.memset(VT[:], 1.0), writes=[("VT", i) for i in range(4)])
    S.op("dve", lambda e: e.memset(MV[:], 1.0), writes=["MV"])
    S.op("dve", lambda e: e.memset(U[:], 0.0), writes=["U"])
    S.op("dve", lambda e: e.memset(US[:], 0.0), writes=["US"])
    S.dma("sp", lambda e: e.dma_start(out=GAINS[:], in_=gains.rearrange("p (a l k) -> p a l k", a=3, l=DEPTH)),
          writes=["GAINS"], key=("init", 1))
    S.dma("sp", lambda e: e.dma_start(out=CW[:], in_=convw.rearrange("p (l i j) -> p l i j", l=DEPTH, i=2)),
          writes=["CW"], key=("init", 2))
    S.dma("sp", lambda e: e.dma_start(out=ESINK[:], in_=sinks), writes=["ESINK"], key=("init", 3))
    S.dma("pool", lambda e: e.dma_start(out=MASKS[:], in_=masks.rearrange("p (a b) -> p a b", a=5)),
          writes=["MASKS"], key=("init", 4))
    S.dma("pool", lambda e: e.dma_start(out=IDB[:], in_=ident), writes=["IDB"], key=("init", 5))
    S.dma("sp", lambda e: e.dma_start(out=IDF[:], in_=ident), writes=["IDF"], key=("init", 6))
    S.op("act", lambda e: e.activation(out=ESINK[:], in_=ESINK[:], func=AF.Exp), reads=["ESINK"], writes=["ESINK"])

    xp_v = xp_T.rearrange("(k p) t -> p k t", p=128)
    xs_v = xs_T.rearrange("(k p) t -> p k t", p=128)

    def load_x(c):
        if c < NPC:
            S.dma("sp", lambda e: e.dma_start(out=X[:, :, c * T:(c + 1) * T], in_=xp_v[:, :, c * T:(c + 1) * T]),
                  writes=[("X", c * T)], key=("x", c))
        else:
            S.dma("sp", lambda e: e.dma_start(out=X[:, :, XS0:XS0 + TS], in_=xs_v), writes=[("X", XS0)], key=("x", NPC))

    def load_rest_inputs():
        for c in range(1, NPC + 1):
            load_x(c)
        for l in range(depth):
            S.dma("sp", lambda e: e.dma_start(out=wk_s[l, :, 0:120, :], in_=ckw[l, :, 8:128, :]), key="cpyk", final=True)
            S.dma("sp", lambda e: e.dma_start(out=wv_s[l, :, 0:120, :], in_=cvw[l, :, 8:128, :]), key="cpyv", final=True)

    load_x(0)

    WI_PIECES = ((1024, 1792), (2304, 2816), (0, 1024), (1792, 2304))

    def wi_piece(col):
        for pi_, (c0, c1) in enumerate(WI_PIECES):
            if c0 <= col < c1:
                return pi_
        raise ValueError(col)

    def load_wi_piece(l, pi_, k):
        c0, c1 = WI_PIECES[pi_]
        if l == 0:
            S.dma("pool", lambda e: e.dma_start(out=WI[:, k, c0:c1], in_=w_in[l, k * 128:(k + 1) * 128, c0:c1]),
                  writes=[("WI", pi_)], key=("WI", pi_, k))
        else:
            S.dma("sp", lambda e: e.dma_start(out=WI[:, k, c0:c1], in_=win_bf[l - 1, k * 128:(k + 1) * 128, c0:c1]),
                  reads=[("WSCR", l)], writes=[("WI", pi_)], key=("WI", pi_, k))

    def precast(l, after):
        for k in range(8):
            for c0, c1 in ((0, 1024), (1024, 2048), (2048, IN_DIM)):
                S.dma("pool", lambda e: e.dma_start(out=win_bf[l - 1, k * 128:(k + 1) * 128, c0:c1],
                                                    in_=w_in[l, k * 128:(k + 1) * 128, c0:c1]),
                      reads=[after], writes=[("WSCR", l)], key=("pc", l))
            S.dma("pool", lambda e: e.dma_start(out=wout_bf[l - 1, k * 128:(k + 1) * 128, :], in_=w_out[l, k * 128:(k + 1) * 128, :]),
                  reads=[after], writes=[("WSCR", l)], key=("pc", l))
        S.seal(("pc", l))

    def load_wi(l, k):
        for pi_ in range(4):
            load_wi_piece(l, pi_, k)

    def load_wo(l):
        for k in range(8):
            if l == 0:
                S.dma("pool", lambda e, k=k: e.dma_start(out=WO[:, k, :], in_=w_out[l, k * 128:(k + 1) * 128, :]),
                      writes=["WO"], key=("WOk", k))
            else:
                S.dma("sp", lambda e, k=k: e.dma_start(out=WO[:, k, :], in_=wout_bf[l - 1, k * 128:(k + 1) * 128, :]),
                      reads=[("WSCR", l)], writes=["WO"], key=("WOk", k))

    def rms_stage(srcs_sq, sqbuf, sqres, rsbuf, rsres, Tn):
        pb, pr = newbank()
        for k in range(8):
            S.op("pe", lambda e, k=k: e.matmul(pb[:, 0:Tn], lhsT=ONES[:, :], rhs=sqbuf[:, k, 0:Tn],
                                                start=(k == 0), stop=(k == 7)),
                 reads=[sqres(k), "ONES"], writes=[pr])
        S.op("act", lambda e: e.activation(out=rsbuf[:, 0:Tn], in_=pb[:, 0:Tn], func=AF.Ln,
                                           bias=EPS[:, 0:1], scale=1.0 / DM),
             reads=[pr, "EPS"], writes=[rsres])
        S.op("act", lambda e: e.activation(out=rsbuf[:, 0:Tn], in_=rsbuf[:, 0:Tn], func=AF.Exp, scale=-0.5),
             reads=[rsres], writes=[rsres])

    def stage_A(l, xc0, Tn, part="all"):
        xr = ("X", xc0)
        for hh in range(2 if part in ("all", "sq") else 0):
            S.op("act", lambda e, hh=hh: e.activation(out=SQ[:, 4 * hh:4 * hh + 4, 0:Tn],
                                                      in_=X[:, 4 * hh:4 * hh + 4, xc0:xc0 + Tn], func=AF.Square),
                 reads=[xr], writes=[("SQ", hh)])
        if part == "sq":
            return
        rms_stage(None, SQ, lambda k: ("SQ", k // 4), RS, "RS", Tn)
        for k in range(8):
            S.op("dve", lambda e, k=k: e.scalar_tensor_tensor(out=H[:, k, 0:Tn], in0=X[:, k, xc0:xc0 + Tn],
                                                              scalar=GAINS[:, 0, l, k:k + 1], in1=RS[:, 0:Tn],
                                                              op0=ALU.mult, op1=ALU.mult),
                 reads=[xr, "RS", "GAINS"], writes=[("H", k)])

    def proj_fm(m, bank, bres, off, Tn):
        for k in range(8):
            S.op("pe", lambda e, k=k: e.matmul(bank[:, off:off + Tn], lhsT=WI[:, k, m * 128:(m + 1) * 128],
                                                rhs=H[:, k, 0:Tn], start=(k == 0), stop=(k == 7)),
                 reads=[("WI", wi_piece(m * 128)), ("H", k)], writes=[bres])

    def proj_tm(col0, ncols, bank, bres, off, tok0):
        for k in range(8):
            S.op("pe", lambda e, k=k: e.matmul(bank[:, off:off + ncols], lhsT=H[:, k, tok0:tok0 + 128],
                                                rhs=WI[:, k, col0:col0 + ncols], start=(k == 0), stop=(k == 7)),
                 reads=[("WI", wi_piece(col0)), ("H", k)], writes=[bres])

    def pair_fm(m0, Tn):
        bank, bres = newbank()
        proj_fm(m0, bank, bres, 0, Tn)
        proj_fm(m0 + 1, bank, bres, Tn, Tn)
        return bank, bres

    def stage_B1(l, Tn, slots, mode, sample=False, kvout=None, part="all"):
        nb = Tn // 128
        full = (mode == "full")
        if part == "b":
            if full:
                bank, bres = pair_fm(10, Tn)
                S.op("dve", lambda e: e.tensor_copy(out=QT[:, 2:4, 0:Tn], in_=v3(bank[:, 0:2 * Tn], 2)),
                     reads=[bres], writes=["QT"])
                bank, bres = pair_fm(18, Tn)
                S.op("dve", lambda e: e.tensor_copy(out=MQ[:, :, 0:Tn], in_=v3(bank[:, 0:2 * Tn], 2)),
                     reads=[bres], writes=["MQ"])
            return
        bank, bres = newbank()
        proj_fm(12, bank, bres, 0, Tn)
        for b in range(nb):
            proj_tm(1664, 128, bank, bres, 256 + b * 128, b * 128)
        for b in range(nb):
            sl = slots[b]
            S.op("dve", lambda e: e.tensor_copy(out=KT[:, sl, :], in_=bank[:, b * 128:(b + 1) * 128]),
                 reads=[bres], writes=[("KT", sl)])
            S.op("act", lambda e: e.activation(out=VT[:, sl, 0, 0:64],
                                               in_=bank[:, 256 + b * 128:256 + b * 128 + 64], func=AF.Copy),
                 reads=[bres], writes=[("VT", sl)])
            S.op("act", lambda e: e.activation(out=VT[:, sl, 1, 64:128],
                                               in_=bank[:, 256 + b * 128 + 64:256 + (b + 1) * 128], func=AF.Copy),
                 reads=[bres], writes=[("VT", sl)])
        if kvout is not None:
            bank, bres = newbank()
            tok0 = Tn - 128
            proj_tm(1536, 256, bank, bres, 0, tok0)
            S.op("act", lambda e: e.activation(out=RC[:, 0:256], in_=bank[:, 0:256], func=AF.Copy),
                 reads=[bres], writes=[("RC", 0), ("RC", 1)])
            kvout()
        if full:
            for jj in range(2 if part == "all" else 1):
                bank, bres = pair_fm(8 + 2 * jj, Tn)
                S.op("dve", lambda e: e.tensor_copy(out=QT[:, 2 * jj:2 * jj + 2, 0:Tn], in_=v3(bank[:, 0:2 * Tn], 2)),
                     reads=[bres], writes=["QT"])
            if part == "all":
                bank, bres = pair_fm(18, Tn)
                S.op("dve", lambda e: e.tensor_copy(out=MQ[:, :, 0:Tn], in_=v3(bank[:, 0:2 * Tn], 2)),
                     reads=[bres], writes=["MQ"])

    def stage_B2(l, Tn, mode, sample=False):
        full = (mode == "full")
        if full:
            bank, bres = pair_fm(0, Tn)
            S.op("act", lambda e: e.activation(out=CB[:, :, 0:Tn], in_=v3(bank[:, 0:2 * Tn], 2), func=AF.Copy),
                 reads=[bres], writes=["CB"])
        bank, bres = pair_fm(2, Tn)
        S.op("act", lambda e: e.activation(out=SG[:, :, 0:Tn], in_=v3(bank[:, 0:2 * Tn], 2), func=AF.Copy),
             reads=[bres], writes=["SG"])
        bank, bres = pair_fm(4, Tn)
        if not sample:
            S.op("dve", lambda e: e.tensor_copy(out=U[:, :, 0:2], in_=U[:, :, T:T + 2]), reads=["U"], writes=["U"])
            S.op("dve", lambda e: e.tensor_tensor(out=U[:, :, 2:2 + Tn], in0=v3(bank[:, 0:2 * Tn], 2),
                                                  in1=SG[:, :, 0:Tn], op=ALU.mult),
                 reads=[bres, "SG"], writes=["U"])
            ub, ures = U, "U"
        else:
            S.op("dve", lambda e: e.tensor_tensor(out=US[:, :, 32:160], in0=v3(bank[:, 0:2 * Tn], 2),
                                                  in1=SG[:, :, 0:Tn], op=ALU.mult),
                 reads=[bres, "SG"], writes=["US"])
            ub, ures = US, "US"
        if not full:
            return
        bank, bres = pair_fm(6, Tn)
        S.op("act", lambda e: e.activation(out=SG[:, :, 0:Tn], in_=v3(bank[:, 0:2 * Tn], 2), func=AF.Silu),
             reads=[bres], writes=["SG"])
        sh = 16 if sample else 1
        for i in range(2):
            S.op("dve", lambda e: e.tensor_scalar(out=ACC[:, i, 0:Tn], in0=ub[:, i, 2 * sh:2 * sh + Tn],
                                                  scalar1=CW[:, l, i, 2:3], scalar2=None, op0=ALU.mult),
                 reads=[ures, "CW"], writes=["ACC"])
            S.op("dve", lambda e: e.scalar_tensor_tensor(out=ACC[:, i, 0:Tn], in0=ub[:, i, sh:sh + Tn],
                                                         scalar=CW[:, l, i, 1:2], in1=ACC[:, i, 0:Tn],
                                                         op0=ALU.mult, op1=ALU.add),
                 reads=[ures, "CW", "ACC"], writes=["ACC"])
            S.op("dve", lambda e: e.scalar_tensor_tensor(out=ACC[:, i, 0:Tn], in0=ub[:, i, 0:Tn],
                                                         scalar=CW[:, l, i, 0:1], in1=ACC[:, i, 0:Tn],
                                                         op0=ALU.mult, op1=ALU.add),
                 reads=[ures, "CW", "ACC"], writes=["ACC"])
        S.op("dve", lambda e: e.tensor_tensor(out=ACC[:, :, 0:Tn], in0=ACC[:, :, 0:Tn], in1=CB[:, :, 0:Tn], op=ALU.mult),
             reads=["ACC", "CB"], writes=["ACC"])
        S.op("dve", lambda e: e.tensor_tensor(out=MIX[:, 0:2, 0:Tn], in0=ACC[:, :, 0:Tn], in1=SG[:, :, 0:Tn], op=ALU.mult),
             reads=["ACC", "SG"], writes=[("MIX", 0)])
        for jj in range(2):
            bank, bres = pair_fm(14 + 2 * jj, Tn)
            S.op("act", lambda e: e.activation(out=SAG[:, 2 * jj:2 * jj + 2, 0:Tn], in_=v3(bank[:, 0:2 * Tn], 2), func=AF.Silu),
                 reads=[bres], writes=["SAG"])
        bank, bres = pair_fm(20, Tn)
        S.op("act", lambda e: e.activation(out=SMG[:, :, 0:Tn], in_=v3(bank[:, 0:2 * Tn], 2), func=AF.Silu),
             reads=[bres], writes=["SMG"])

    def rows(g):
        return (slice(0, 64), slice(64, 128)) if g == 0 else (slice(64, 128), slice(0, 64))

    def finish_heads(l, g, bank_o, bres, ncol, sink, gate, gres, mixk0, nh, tok0, tokn, gate_eng="dve"):
        orows, srows = rows(g)
        if sink:
            for j in range(nh):
                S.op("act", lambda e, j=j: e.activation(out=RC[srows, j * tokn:(j + 1) * tokn],
                                                        in_=bank_o[srows, j * tokn:(j + 1) * tokn], func=AF.Ln,
                                                        bias=ESINK[srows, l * 8 + 4 * g + j:l * 8 + 4 * g + j + 1]),
                     reads=[bres, "ESINK"], writes=[("RC", g)])
        else:
            S.op("act", lambda e: e.activation(out=RC[srows, 0:ncol], in_=bank_o[srows, 0:ncol], func=AF.Ln),
                 reads=[bres], writes=[("RC", g)])
        S.op("act", lambda e: e.activation(out=RC[srows, 0:ncol], in_=RC[srows, 0:ncol], func=AF.Exp, scale=-1.0),
             reads=[("RC", g)], writes=[("RC", g)])
        S.op("dve", lambda e: e.tensor_tensor(out=TMP[orows, 0:ncol], in0=bank_o[orows, 0:ncol], in1=RC[srows, 0:ncol], op=ALU.mult),
             reads=[bres, ("RC", g)], writes=[("TMP", g)])
        S.op(gate_eng, lambda e: e.tensor_tensor(out=MIX[orows, mixk0:mixk0 + nh, tok0:tok0 + tokn],
                                              in0=v3(TMP[orows, 0:ncol], nh),
                                              in1=gate[orows, 0:nh, tok0:tok0 + tokn], op=ALU.mult),
             reads=[("TMP", g), gres], writes=[("MIX", 1 + g)])

    def score_exp_mask(lhsT, rhs, lres, mask_idx):
        bank, bres = newbank()
        S.op("pe", lambda e: e.matmul(bank[:, 0:512], lhsT=lhsT, rhs=rhs, start=True, stop=True),
             reads=lres, writes=[bres])
        pt, pres = newpt()
        S.op("act", lambda e: e.activation(out=pt[:, 0:512], in_=bank[:, 0:512], func=AF.Exp, scale=0.125),
             reads=[bres], writes=[pres])
        if mask_idx is not None:
            S.op("dve", lambda e: e.tensor_tensor(out=v3(pt[:, 0:512], 4), in0=v3(pt[:, 0:512], 4),
                                                  in1=MASKS[:, mask_idx, :].unsqueeze(1).broadcast_to([128, 4, 128]),
                                                  op=ALU.mult),
                 reads=[pres, "MASKS"], writes=[pres])
        return pt, pres

    def stage_C_prompt(l, c, slots):
        nb = T // 128

        def win_scores(b):
            sl_prev = slots[b - 1] if b > 0 else (slots[0] - 1) % 4
            sl_own = slots[b]
            first = (c == NHALO and b == 0)
            tiles = {}
            for kb, sl in ((0, sl_prev), (1, sl_own)):
                for g in range(2):
                    gs = slice(64 * g, 64 * g + 64)
                    bank, bres = newbank()
                    S.op("pe", lambda e: e.matmul(bank[:, 0:512], lhsT=KT[gs, sl, :], rhs=QT[gs, 0:4, b * 128:(b + 1) * 128],
                                                  start=True, stop=True),
                         reads=[("KT", sl), "QT"], writes=[bres])
                    tiles[(kb, g)] = (bank, bres, sl)
            out = {0: [], 1: []}
            for kb in range(2):
                for g in range(2):
                    bank, bres, sl = tiles[(kb, g)]
                    pt, pres = newpt()
                    S.op("act", lambda e: e.activation(out=pt[:, 0:512], in_=bank[:, 0:512], func=AF.Exp, scale=0.125),
                         reads=[bres], writes=[pres])
                    midx = 0 if kb == 1 else (2 if first else 1)
                    S.op("dve", lambda e: e.tensor_tensor(out=v3(pt[:, 0:512], 4), in0=v3(pt[:, 0:512], 4),
                                                          in1=MASKS[:, midx, :].unsqueeze(1).broadcast_to([128, 4, 128]),
                                                          op=ALU.mult),
                         reads=[pres, "MASKS"], writes=[pres])
                    out[g].append((pt, pres, sl))
            return out

        def win_pv(b, g, pts):
            bank_o, bres = newbank()
            for n, (pt, pres, sl) in enumerate(pts):
                S.op("pe", lambda e: e.matmul(bank_o[:, 0:512], lhsT=VT[:, sl, g, :], rhs=pt[:, 0:512],
                                              start=(n == 0), stop=(n == 1)),
                     reads=[("VT", sl), pres], writes=[bres])
            finish_heads(l, g, bank_o, bres, 512, True, SAG, "SAG", 2, 4, b * 128, 128)

        def mem_scores(r):
            rs_ = slice(64 * r, 64 * r + 64)
            pts = []
            for mb in range(2):
                bank, bres = newbank()
                for i in range(2):
                    S.op("pe", lambda e: e.matmul(bank[:, i * T:(i + 1) * T], lhsT=MKT[rs_, i, mb * 128:(mb + 1) * 128],
                                                  rhs=MQ[rs_, i, 0:T], start=True, stop=True),
                         reads=["MKT", "MQ"], writes=[bres])
                pt, pres = newpt()
                S.op("act", lambda e: e.activation(out=pt[:, 0:512], in_=bank[:, 0:512], func=AF.Exp, scale=0.125),
                     reads=[bres], writes=[pres])
                pts.append((pt, pres))
            return pts

        def mem_pv(r, pts):
            bank_o, bres = newbank()
            for i in range(2):
                for mb in range(2):
                    pt, pres = pts[mb]
                    S.op("pe", lambda e: e.matmul(bank_o[:, i * T:(i + 1) * T], lhsT=MV[:, mb, 2 * i + r, :],
                                                  rhs=pt[:, i * T:(i + 1) * T], start=(mb == 0), stop=(mb == 1)),
                         reads=["MV", pres], writes=[bres])
            finish_heads(l, r, bank_o, bres, 2 * T, False, SMG, "SMG", 6, 2, 0, T)

        w0 = win_scores(0)
        m0 = mem_scores(0)

        def rest():
            win_pv(0, 0, w0[0])
            win_pv(0, 1, w0[1])
            mem_pv(0, m0)
            w1 = win_scores(1)
            m1 = mem_scores(1)
            win_pv(1, 0, w1[0])
            win_pv(1, 1, w1[1])
            mem_pv(1, m1)
        return rest

    def stage_D(l, xc0, Tn):
        xr = ("X", xc0)
        mixres = [("MIX", 0), ("MIX", 1), ("MIX", 2)]
        for n2 in range(4):
            bank, bres = newbank()
            for q in range(2):
                n = 2 * n2 + q
                for k in range(8):
                    S.op("pe", lambda e, k=k, n=n, q=q, bank=bank: e.matmul(bank[:, q * Tn:(q + 1) * Tn],
                                                                            lhsT=WO[:, k, n * 128:(n + 1) * 128],
                                                                            rhs=MIX[:, k, 0:Tn], start=(k == 0), stop=(k == 7)),
                         reads=["WO"] + mixres, writes=[bres])
            S.op("act", lambda e, bank=bank, n2=n2: e.activation(out=Y[:, 2 * n2:2 * n2 + 2, 0:Tn], in_=v3(bank[:, 0:2 * Tn], 2), func=AF.Copy),
                 reads=[bres], writes=[("Y", n2)])
            S.op("dve", lambda e, n2=n2: e.tensor_tensor(out=YSQ[:, 2 * n2:2 * n2 + 2, 0:Tn], in0=Y[:, 2 * n2:2 * n2 + 2, 0:Tn],
                                                         in1=Y[:, 2 * n2:2 * n2 + 2, 0:Tn], op=ALU.mult),
                 reads=[("Y", n2)], writes=[("SQ", n2 // 2)])
        rms_stage(None, YSQ, lambda k: ("SQ", k // 4), RS2, "RS2", Tn)
        for n in range(8):
            S.op("dve", lambda e, n=n: e.scalar_tensor_tensor(out=Y[:, n, 0:Tn], in0=Y[:, n, 0:Tn],
                                                              scalar=GAINS[:, 1, l, n:n + 1], in1=RS2[:, 0:Tn],
                                                              op0=ALU.mult, op1=ALU.mult),
                 reads=[("Y", n // 2), "RS2", "GAINS"], writes=[("Y", n // 2)])
        S.op("dve", lambda e: e.tensor_tensor(out=X[:, :, xc0:xc0 + Tn], in0=X[:, :, xc0:xc0 + Tn], in1=Y[:, :, 0:Tn], op=ALU.add),
             reads=[xr] + [("Y", i) for i in range(4)], writes=[xr])

    def stage_mem(l):
        yres = [("Y", i) for i in range(4)]
        S.dma("sp", lambda e: e.dma_start(out=Y[:, :, :], in_=mem_T.rearrange("(k p) t -> p k t", p=128)),
              writes=yres, key="MEMT")
        for k in range(8):
            S.dma("pool", lambda e, k=k: e.dma_start(out=WO[:, k // 2, (k % 2) * 512:(k % 2 + 1) * 512],
                                                     in_=w_mem[l, k * 128:(k + 1) * 128, :]),
                  writes=["WO"], key=("WM", k))
        import os
        MS = int(os.environ.get("MK_MEMSTEP", "99"))
        if MS < 2:
            return
        for hh in range(2):
            S.op("act", lambda e, hh=hh: e.activation(out=SQ[:, 4 * hh:4 * hh + 4, :], in_=Y[:, 4 * hh:4 * hh + 4, :], func=AF.Square),
                 reads=yres, writes=[("SQ", hh)])
        if MS < 3:
            return
        rms_stage(None, SQ, lambda k: ("SQ", k // 4), RS, "RS", 256)
        if MS < 4:
            return
        for k in range(8):
            S.op("dve", lambda e, k=k: e.scalar_tensor_tensor(out=H[:, k, :], in0=Y[:, k, :], scalar=GAINS[:, 2, l, k:k + 1],
                                                              in1=RS[:, :], op0=ALU.mult, op1=ALU.mult),
                 reads=yres + ["RS", "GAINS"], writes=[("H", k)])

        def wm(k):
            return WO[:, k // 2, (k % 2) * 512:(k % 2 + 1) * 512]
        if MS < 5:
            return
        for mb in range(2):
            bank, bres = newbank()
            for k in range(8):
                S.op("pe", lambda e, k=k, bank=bank: e.matmul(bank[:, 0:512], lhsT=H[:, k, mb * 128:(mb + 1) * 128], rhs=wm(k),
                                                               start=(k == 0), stop=(k == 7)),
                     reads=["WO", ("H", k)], writes=[bres])
            accf = ACC[:].rearrange("p a b -> p (a b)")
            S.op("act", lambda e, bank=bank: e.activation(out=accf, in_=bank[:, 0:512], func=AF.Copy), reads=[bres], writes=["ACC"])
            if MS < 6:
                continue
            bv = bank[:, 256:512].rearrange("p (i r d) -> p i r d", i=2, r=2)
            S.op("dve", lambda e: e.tensor_copy(out=MV[:, mb, :, 0:64].rearrange("p (i r) d -> p i r d", r=2)[:, :, 0, :], in_=bv[:, :, 0, :]),
                 reads=[bres], writes=["MV"])
            S.op("dve", lambda e: e.tensor_copy(out=MV[:, mb, :, 64:128].rearrange("p (i r) d -> p i r d", r=2)[:, :, 1, :], in_=bv[:, :, 1, :]),
                 reads=[bres], writes=["MV"])
            S.dma("sp", lambda e: e.dma_start(out=mk_p[l, mb * 128:(mb + 1) * 128, :], in_=accf[:, 0:256]), reads=["ACC"], key="mkvo", final=True)
            S.dma("sp", lambda e: e.dma_start(out=mv_p[l, mb * 128:(mb + 1) * 128, :], in_=accf[:, 256:512]), reads=["ACC"], key="mkvo", final=True)
        if MS < 7:
            return
        bank, bres = newbank()
        for i in range(2):
            for k in range(8):
                lw = wm(k)[:, 128 * i:128 * i + 128]
                S.op("pe", lambda e, k=k, i=i, lw=lw: e.matmul(bank[:, i * 256:(i + 1) * 256], lhsT=lw, rhs=H[:, k, :],
                                                               start=(k == 0), stop=(k == 7)),
                     reads=["WO", ("H", k)], writes=[bres])
        S.op("dve", lambda e: e.tensor_copy(out=MKT[:, :, :], in_=v3(bank[:, 0:512], 2)), reads=[bres], writes=["MKT"])

    def stage_C_sample(l, slot, next_wi):
        ring["list"] = list(range(5))
        ring["pos"] = 0
        ptring["n"] = NPT
        ptpos[0] = 0
        bog = [(banks[5], ("PS", 5)), (banks[6], ("PS", 6))]
        bom, bomres = banks[7], ("PS", 7)
        for g in range(2):
            gs = slice(64 * g, 64 * g + 64)
            pt, pres = score_exp_mask(KT[gs, slot, :], QT[gs, 0:4, 0:128], [("KT", slot), "QT"], 3)
            bo, bor = bog[g]
            S.op("pe", lambda e, bo=bo, pt=pt: e.matmul(bo[:, 0:512], lhsT=VT[:, slot, g, :], rhs=pt[:, 0:512], start=True, stop=False),
                 reads=[("VT", slot), pres], writes=[bor])
        def k_dma(s):
            par = s % 2
            S.dma("sp", lambda e: e.dma_start(out=KCS[par][:], in_=ckw[l, s]), writes=[("KCS", par)], key=("kc", par))
            S.dma("sp", lambda e: e.dma_start(out=MKS[par][:], in_=cmk[l, s].rearrange("(p b) c -> p b c", b=2)),
                  writes=[("MKS", par)], key=("mk", par))

        def v_dma(s):
            par = s % 2
            S.dma("pool", lambda e: e.dma_start(out=VC[par][:], in_=cvw[l, s]), writes=[("VC", par)], key=("vc", par))
            S.dma("pool", lambda e: e.dma_start(out=MVS[par][:], in_=cmv[l, s].rearrange("(p b) c -> p b c", b=2)),
                  writes=[("MVS", par)], key=("mv", par))

        def tr(s):
            par = s % 2
            ba, bares = newbank()
            bb, bbres = newbank()
            S.op("pe", lambda e: e.transpose(out=ba[:, 0:128], in_=KCS[par][:, :], identity=IDF[:, :]),
                 reads=[("KCS", par), "IDF"], writes=[bares])
            n = 0
            for i in range(2):
                for mb in range(2):
                    src = MKS[par][:, mb, 128 * i:128 * i + 128]
                    n += 1
                    if n < 4:
                        S.op("pe", lambda e: e.transpose(out=ba[:, n * 128:(n + 1) * 128], in_=src, identity=IDF[:, :]),
                             reads=[("MKS", par), "IDF"], writes=[bares])
                    else:
                        S.op("pe", lambda e: e.transpose(out=bb[:, 0:128], in_=src, identity=IDF[:, :]),
                             reads=[("MKS", par), "IDF"], writes=[bbres])
            mflat = MKTS[par][:].rearrange("p a b c -> p (a b c)")
            S.op("dve", lambda e: e.tensor_copy(out=KCT[par][:, :], in_=ba[:, 0:128]), reads=[bares], writes=[("KCT", par)])
            S.op("dve", lambda e: e.tensor_copy(out=mflat[:, 0:384], in_=ba[:, 128:512]), reads=[bares], writes=[("MKTS", par)])
            S.op("dve", lambda e: e.tensor_copy(out=mflat[:, 384:512], in_=bb[:, 0:128]), reads=[bbres], writes=[("MKTS", par)])

        def sc(s):
            par = s % 2
            bsb = [newbank(), newbank()]
            for g in range(2):
                gs = slice(64 * g, 64 * g + 64)
                bk, bkres = bsb[g]
                S.op("pe", lambda e: e.matmul(v3(bk[:, 0:32], 4), lhsT=KCT[par][gs, :],
                                              rhs=QT[gs, 0:4, s:128:16], start=True, stop=True),
                     reads=[("KCT", par), "QT"], writes=[bkres])
            for r in range(2):
                rs_ = slice(64 * r, 64 * r + 64)
                bk, bkres = bsb[r]
                for i in range(2):
                    for mb in range(2):
                        o0 = 32 + (i * 2 + mb) * 8
                        S.op("pe", lambda e: e.matmul(bk[:, o0:o0 + 8], lhsT=MKTS[par][rs_, i, mb, :],
                                                      rhs=MQ[rs_, i, s:128:16], start=True, stop=True),
                             reads=[("MKTS", par), "MQ"], writes=[bkres])
            return bsb

        def ex(s, bsb):
            par = s % 2
            for rg in range(2):
                bk, bkres = bsb[rg]
                S.op("act", lambda e: e.activation(out=PSS[par][:, 64 * rg:64 * rg + 64], in_=bk[:, 0:64], func=AF.Exp, scale=0.125),
                     reads=[bkres], writes=[("PSS", par)])
            pw = PSS[par][:, :].rearrange("p (g x) -> p g x", g=2)[:, :, 0:32].rearrange("p g (j t) -> p g j t", j=4)
            S.op("dve", lambda e: e.tensor_tensor(out=pw, in0=pw,
                                                  in1=MASKS[:, 4, 0:8].unsqueeze(1).unsqueeze(1).broadcast_to([128, 2, 4, 8]), op=ALU.mult),
                 reads=[("PSS", par), "MASKS"], writes=[("PSS", par)])

        def pv(s):
            par = s % 2
            for g in range(2):
                orows, srows = rows(g)
                bo, bor = bog[g]
                ov = bo[orows, 0:512].rearrange("p (j t s) -> p j t s", j=4, t=8)[:, :, :, s]
                sv = bo[srows, 0:512].rearrange("p (j t s) -> p j t s", j=4, t=8)[:, :, :, s]
                rh = v3(PSS[par][:, 64 * g:64 * g + 32], 4)
                S.op("pe", lambda e: e.matmul(ov, lhsT=VC[par][:, g * 64:(g + 1) * 64], rhs=rh, start=False, stop=False),
                     reads=[("VC", par), ("PSS", par)], writes=[bor])
                S.op("pe", lambda e: e.matmul(sv, lhsT=ONES[:, 0:64], rhs=rh, start=False, stop=(s == NSEQ - 1)),
                     reads=["ONES", ("PSS", par)], writes=[bor])
            for r in range(2):
                orows, srows = rows(r)
                for i in range(2):
                    h = 2 * i + r
                    c0 = (r * 2 + i) * 128
                    ov = bom[orows, c0:c0 + 128].rearrange("p (t s) -> p t s", t=8)[:, :, s]
                    sv = bom[srows, c0:c0 + 128].rearrange("p (t s) -> p t s", t=8)[:, :, s]
                    for mb in range(2):
                        o0 = 64 * r + 32 + (i * 2 + mb) * 8
                        S.op("pe", lambda e: e.matmul(ov, lhsT=MVS[par][:, mb, h * 64:(h + 1) * 64],
                                                      rhs=PSS[par][:, o0:o0 + 8], start=(mb == 0), stop=(mb == 1)),
                             reads=[("MVS", par), ("PSS", par)], writes=[bomres])
                        S.op("pe", lambda e: e.matmul(sv, lhsT=ONES[:, 0:64], rhs=PSS[par][:, o0:o0 + 8],
                                                      start=(mb == 0), stop=(mb == 1)),
                             reads=["ONES", ("PSS", par)], writes=[bomres])

        k_dma(0)
        k_dma(1)
        v_dma(0)
        tr(0)
        for s in range(NSEQ):
            if s + 2 < NSEQ:
                k_dma(s + 2)
            if s + 1 < NSEQ:
                tr(s + 1)
            bsb = sc(s)
            if s >= 1:
                pv(s - 1)
            if s + 1 < NSEQ:
                v_dma(s + 1)
            ex(s, bsb)
            if next_wi is not None and s % 2 == 1:
                next_wi(s // 2)
        pv(NSEQ - 1)
        for g in range(2):
            bo, bor = bog[g]
            finish_heads(l, g, bo, bor, 512, True, SAG, "SAG", 2, 4, 0, 128)
        for r in range(2):
            orows, srows = rows(r)
            S.op("act", lambda e, r=r, srows=srows: e.activation(out=RC[srows, 0:256], in_=bom[srows, r * 256:(r + 1) * 256], func=AF.Ln),
                 reads=[bomres], writes=[("RC", r)])
            S.op("act", lambda e, srows=srows: e.activation(out=RC[srows, 0:256], in_=RC[srows, 0:256], func=AF.Exp, scale=-1.0),
                 reads=[("RC", r)], writes=[("RC", r)])
            S.op("dve", lambda e, r=r, orows=orows, srows=srows: e.tensor_tensor(out=TMP[orows, 0:256], in0=bom[orows, r * 256:(r + 1) * 256],
                                                                                 in1=RC[srows, 0:256], op=ALU.mult),
                 reads=[bomres, ("RC", r)], writes=[("TMP", r)])
            S.op("dve", lambda e, orows=orows: e.tensor_tensor(out=MIX[orows, 6:8, 0:128], in0=v3(TMP[orows, 0:256], 2),
                                                               in1=SMG[orows, 0:2, 0:128], op=ALU.mult),
                 reads=[("TMP", r), "SMG"], writes=[("MIX", 1 + r)])
        ring["list"] = list(range(8))
        ring["pos"] = 0
        ptring["n"] = 8
        ptpos[0] = 0

    import os
    LIMIT = int(os.environ.get("MK_LIMIT", "99"))
    NCH = int(os.environ.get("MK_NCH", str(NPC)))

    def program():
        if LIMIT < 1:
            return
        SKIP = os.environ.get("MK_SKIP", "")
        for l in range(depth):
            if "mem" not in SKIP:
                stage_mem(l)
            if l == 0:
                load_rest_inputs()
                for pi_ in range(4):
                    for k in range(8):
                        load_wi_piece(0, pi_, k)
            if "wo" not in SKIP:
                load_wo(l)
            if LIMIT < 2:
                return
            last = (l == depth - 1)
            plan = []
            for c in range(NPC):
                if c >= NCH and c < NPC - 1:
                    continue
                if c < NHALO:
                    lv = depth - 1 - l
                    if c == 1:
                        mode = "full" if lv >= 1 else "kv"
                    else:
                        mode = "full" if lv >= 3 else ("kv" if lv == 2 else "skip")
                else:
                    mode = "full"
                if mode != "skip":
                    plan.append((c, mode))
            S.dma("sp", lambda e: e.dma_start(out=US[:, :, 0:32], in_=sconv[l].rearrange("(i p) c -> p i c", p=128)),
                  writes=["US"], key="sconv")

            def kvout_p(l=l):
                S.dma("sp", lambda e: e.dma_start(out=wk_p[l], in_=RC[:, 0:128]), reads=[("RC", 0), ("RC", 1)], key="kvo", final=True)
                S.dma("sp", lambda e: e.dma_start(out=wv_p[l], in_=RC[:, 128:256]), reads=[("RC", 0), ("RC", 1)], key="kvo", final=True)

            def kvout_s(l=l):
                for t in range(8):
                    S.dma("sp", lambda e: e.dma_start(out=wk_s[l, :, 120 + t, :], in_=RC[16 * t:16 * t + 16, 0:128]),
                          reads=[("RC", 0), ("RC", 1)], key="kvo", final=True)
                    S.dma("sp", lambda e: e.dma_start(out=wv_s[l, :, 120 + t, :], in_=RC[16 * t:16 * t + 16, 128:256]),
                          reads=[("RC", 0), ("RC", 1)], key="kvo", final=True)

            def slots_of(c):
                return [(2 * c + 1) % 4, (2 * c + 2) % 4]
            sslot = 1

            def b1_of(pi, part="all"):
                if pi < len(plan):
                    c, mode = plan[pi]
                    stage_B1(l, T, slots_of(c), mode, sample=False, kvout=(kvout_p if c == NPC - 1 else None), part=part)
                else:
                    stage_B1(l, TS, [sslot], "full", sample=True, kvout=kvout_s, part=part)

            stage_A(l, plan[0][0] * T, T)
            b1_of(0)
            for pi, (c, mode) in enumerate(plan):
                xc0 = c * T
                slots = slots_of(c)
                rest = None
                nxt = (plan[pi + 1][0] * T, T) if pi + 1 < len(plan) else (XS0, TS)
                stage_A(l, nxt[0], nxt[1], part="sq")
                if mode == "full":
                    rest = stage_C_prompt(l, c, slots)
                stage_B2(l, T, mode, sample=False)
                if c == NPC - 1:
                    S.dma("sp", lambda e: e.dma_start(out=conv_p[l].rearrange("(i p) j -> p i j", p=128), in_=U[:, :, T:T + 2]),
                          reads=["U"], key="cvo", final=True)
                stage_A(l, nxt[0], nxt[1], part="rest")
                if rest is not None:
                    rest()
                b1_of(pi + 1)
                if mode == "full":
                    stage_D(l, xc0, T)
                if l == 0 and pi in (1, 4, 7) and (pi // 3 + 1) < depth:
                    precast(pi // 3 + 1, ("X", xc0))
            stage_B2(l, TS, "full", sample=True)
            S.dma("sp", lambda e: e.dma_start(out=conv_s[l].rearrange("(i p) c -> p i c", p=128), in_=US[:, :, 128:160]),
                  reads=["US"], key="cvso", final=True)
            stage_C_sample(l, sslot, None)
            if not last:
                for pi_ in range(4):
                    for k in range(8):
                        load_wi_piece(l + 1, pi_, k)
            if LIMIT < 7:
                return
            stage_D(l, XS0, TS)

    program()
    S.dma("sp", lambda e: e.dma_start(out=y_T.rearrange("(k p) t -> p k t", p=128), in_=X[:, :, NHALO * T:NPC * T]),
          reads=[("X", c * T) for c in range(NHALO, NPC)], key="yo", final=True)
    S.dma("sp", lambda e: e.dma_start(out=ys_T.rearrange("(k p) t -> p k t", p=128), in_=X[:, :, XS0:XS0 + TS]),
          reads=[("X", XS0)], key="yso", final=True)
    stats = S.finalize()
    return nc, stats


def _perms():
    pin = list(range(0, 1024))
    pin += [1024 + 64 * (4 * r + j) + d for j in range(4) for r in range(2) for d in range(64)]
    pin += list(range(1536, 1792))
    pin += [1792 + 64 * (4 * r + j) + d for j in range(4) for r in range(2) for d in range(64)]
    pin += list(range(2304, 2816))
    pout = list(range(0, 256))
    pout += [256 + 64 * (4 * r + j) + d for j in range(4) for r in range(2) for d in range(64)]
    pout += list(range(768, 1024))
    return np.array(pin), np.array(pout)


_NC_CACHE = {}


def kernel(x_prompt, x_sample, mem_prompt, cache_win_k, cache_win_v, state_conv,
           cache_mem_k, cache_mem_v, norm_pre, norm_post, norm_mem, w_in, conv_w,
           attn_sinks, w_mem_kv, w_out):
    f = np.float32
    x_prompt = np.asarray(x_prompt, f); x_sample = np.asarray(x_sample, f)
    mem_prompt = np.asarray(mem_prompt, f)
    cache_win_k = np.asarray(cache_win_k, f); cache_win_v = np.asarray(cache_win_v, f)
    state_conv = np.asarray(state_conv, f)
    cache_mem_k = np.asarray(cache_mem_k, f); cache_mem_v = np.asarray(cache_mem_v, f)
    norm_pre = np.asarray(norm_pre, f); norm_post = np.asarray(norm_post, f); norm_mem = np.asarray(norm_mem, f)
    w_in = np.asarray(w_in, f); conv_w = np.asarray(conv_w, f); attn_sinks = np.asarray(attn_sinks, f)
    w_mem_kv = np.asarray(w_mem_kv, f); w_out = np.asarray(w_out, f)

    pin, pout = _perms()
    w_in_p = np.ascontiguousarray(w_in[:, :, pin])
    w_out_p = np.ascontiguousarray(w_out[:, pout, :])
    w_mem_c = np.ascontiguousarray(w_mem_kv)
    gains = np.stack([norm_pre, norm_post, norm_mem], 0).reshape(3, DEPTH, 8, 128)
    gains = np.ascontiguousarray(gains.transpose(3, 0, 1, 2).reshape(128, 3 * DEPTH * 8))
    cw = conv_w.reshape(DEPTH, 3, 2, 128)
    cw = np.ascontiguousarray(cw.transpose(3, 0, 2, 1).reshape(128, DEPTH * 6))
    sinks = np.ascontiguousarray(np.broadcast_to(attn_sinks.reshape(1, DEPTH * 8), (128, DEPTH * 8))).astype(f)
    ident = np.eye(128, dtype=f)
    tk = np.arange(128)[:, None]; tq = np.arange(128)[None, :]
    m_own = (tk <= tq).astype(f)
    m_prev = (tk > tq).astype(f)
    m_snew = (((tk % 16) == (tq % 16)) & ((tk // 16) <= (tq // 16))).astype(f)
    m_sc = np.zeros((128, 128), f)
    m_sc[:, 0:8] = (np.arange(128)[:, None] > np.arange(8)[None, :]).astype(f)

    in_maps = []
    for c in range(NCORES):
        b, q = c // 4, c % 4
        own = x_prompt[b, q * 2048:(q + 1) * 2048]
        halo = x_prompt[b, q * 2048 - 512:q * 2048] if q > 0 else np.zeros((512, DM), f)
        xp_T = np.ascontiguousarray(np.concatenate([halo, own], 0).T)
        xs = x_sample[16 * c:16 * c + 16]
        xs_T = np.ascontiguousarray(xs.transpose(2, 1, 0).reshape(DM, TS))
        m_pf = m_prev if q > 0 else np.zeros((128, 128), f)
        masks = np.ascontiguousarray(np.stack([m_own, m_prev, m_pf, m_snew, m_sc], 1).reshape(128, 5 * 128))
        sc = state_conv[:, 16 * c:16 * c + 16]
        sc = np.ascontiguousarray(sc.transpose(0, 3, 2, 1).reshape(DEPTH, 256, 32))
        in_maps.append({
            "xp_T": xp_T, "xs_T": xs_T,
            "mem_T": np.ascontiguousarray(mem_prompt[b].T),
            "ckw": np.ascontiguousarray(cache_win_k[:, 16 * c:16 * c + 16].reshape(DEPTH, 16, 128, 128)),
            "cvw": np.ascontiguousarray(cache_win_v[:, 16 * c:16 * c + 16].reshape(DEPTH, 16, 128, 128)),
            "cmk": np.ascontiguousarray(cache_mem_k[:, 16 * c:16 * c + 16].reshape(DEPTH, 16, 256, 256)),
            "cmv": np.ascontiguousarray(cache_mem_v[:, 16 * c:16 * c + 16].reshape(DEPTH, 16, 256, 256)),
            "sconv": sc, "gains": gains, "convw": cw, "sinks": sinks,
            "w_in": w_in_p, "w_out": w_out_p, "w_mem": w_mem_c,
            "masks": masks, "ident": ident,
        })

    if "nc" not in _NC_CACHE:
        import os
        _NC_CACHE["nc"] = build_nc(depth=int(os.environ.get("MK_DEPTH", str(DEPTH))))[0]
    nc = _NC_CACHE["nc"]
    res = run_bass_kernel_spmd(nc, in_maps, core_ids=list(range(NCORES)))
    R = res.results

    y_prompt = np.empty((2, 8192, DM), f)
    y_sample = np.empty((128, 8, DM), f)
    wkp = np.empty((DEPTH, 2, 128, 2, 64), f); wvp = np.empty_like(wkp)
    cvp = np.empty((DEPTH, 2, 2, 256), f)
    mkp = np.empty((DEPTH, 2, 256, 4, 64), f); mvp = np.empty_like(mkp)
    wks = np.empty((DEPTH, 128, 128, 2, 64), f); wvs = np.empty_like(wks)
    cvs = np.empty((DEPTH, 128, 2, 256), f)
    for c in range(NCORES):
        b, q = c // 4, c % 4
        r = R[c]
        y_prompt[b, q * 2048:(q + 1) * 2048] = r["y_T"].T
        y_sample[16 * c:16 * c + 16] = r["ys_T"].reshape(DM, 8, 16).transpose(2, 1, 0)
        if q == 3:
            wkp[:, b] = r["wk_p"].reshape(DEPTH, 128, 2, 64)
            wvp[:, b] = r["wv_p"].reshape(DEPTH, 128, 2, 64)
            cvp[:, b] = r["conv_p"].transpose(0, 2, 1)
        if q == 0:
            mkp[:, b] = r["mk_p"].reshape(DEPTH, 256, 4, 64)
            mvp[:, b] = r["mv_p"].reshape(DEPTH, 256, 4, 64)
        wks[:, 16 * c:16 * c + 16] = r["wk_s"].reshape(DEPTH, 16, 128, 2, 64)
        wvs[:, 16 * c:16 * c + 16] = r["wv_s"].reshape(DEPTH, 16, 128, 2, 64)
        cvs[:, 16 * c:16 * c + 16] = r["conv_s"].reshape(DEPTH, 256, 2, 16).transpose(0, 3, 2, 1)
    return (y_prompt, y_sample, wkp, wvp, cvp, mkp, mvp, wks, wvs, cvs)
```
